# Optimizing a Trainium2 kernel written in Bass

```python
import math, functools
import jax, jax.numpy as jnp
from jax import lax
import numpy as np

D_MODEL = 1024
BATCH = 2
SEQ = 8192
DEPTH = 2
DEC_BATCH = 128
DEC_SEQ = 4
PAST_LEN = 2048
PAGE_SIZE = 128

N_BRANCH = 4
W_MIX = D_MODEL // 4
W_SSM = W_MIX
SSM_GROUP = 16
SSM_GROUPS = W_SSM // SSM_GROUP
SSM_P = 64
W_CONV = W_MIX
CONV_B_W = 3
GDN_H = 4
GDN_DK = W_MIX // GDN_H
GDN_DV = W_MIX // GDN_H
GDN_CONV = 4
GDN_CHUNK = 64
W_GDN_QKV = 2 * GDN_H * GDN_DK + GDN_H * GDN_DV
SB_H = 4
SB_DH = W_MIX // SB_H
SB_BLOCK = 128
D_FF = ((8 * D_MODEL + 3 * 256 - 1) // (3 * 256)) * 256
N_IN = W_SSM + 3 * W_CONV + W_GDN_QKV + GDN_H * GDN_DV + 2 * GDN_H + 3 * SB_H * SB_DH
ALPHA_DN = (2 * DEPTH) ** 0.25
BETA_DN = (8 * DEPTH) ** -0.25
LN_EPS = 1e-5
RMS_EPS = 1e-6

kernel_name = 'hybrid_gated_branch_decoder_step'


def layer_norm(x, g=None, b=None):
    xf = x.astype(jnp.float32)
    mu = jnp.mean(xf, axis=-1, keepdims=True)
    var = jnp.mean(jnp.square(xf - mu), axis=-1, keepdims=True)
    y = (xf - mu) * lax.rsqrt(var + LN_EPS)
    if g is not None:
        y = y * g.astype(jnp.float32) + b.astype(jnp.float32)
    return y.astype(x.dtype)


def l2norm(x):
    return x * lax.rsqrt(jnp.sum(jnp.square(x), axis=-1, keepdims=True) + RMS_EPS)


def causal_conv(u, w, buf):
    k_w = w.shape[0]
    L = u.shape[1]
    xc = jnp.concatenate([buf.astype(u.dtype), u], axis=1)
    out = xc[:, 0:L] * w[0]
    for i in range(1, k_w):
        out = out + xc[:, i:i + L] * w[i]
    return out, xc[:, L:]


def cmul(ar, ai, br, bi):
    return ar * br - ai * bi, ar * bi + ai * br


def s5_ssm(u, a_re, a_im, log_dt, b_re, b_im, c_re, c_im, d, h0_re, h0_im):
    f32 = jnp.float32
    bsz, L, _ = u.shape
    uf = u.astype(f32).reshape(bsz, L, SSM_GROUPS, SSM_GROUP)
    a_re = a_re.astype(f32)
    a_im = a_im.astype(f32)
    dt = jnp.exp(log_dt.astype(f32))[:, None]
    mag = jnp.exp(dt * a_re)
    ab_re = mag * jnp.cos(dt * a_im)
    ab_im = mag * jnp.sin(dt * a_im)
    den = a_re * a_re + a_im * a_im
    f_re = ((ab_re - 1.0) * a_re + ab_im * a_im) / den
    f_im = (ab_im * a_re - (ab_re - 1.0) * a_im) / den
    bb_re, bb_im = cmul(f_re[..., None], f_im[..., None], b_re.astype(f32), b_im.astype(f32))
    bu_re = jnp.einsum('blgh,gph->blgp', uf, bb_re)
    bu_im = jnp.einsum('blgh,gph->blgp', uf, bb_im)
    a_b_re = jnp.broadcast_to(ab_re, bu_re.shape)
    a_b_im = jnp.broadcast_to(ab_im, bu_re.shape)

    def combine(e1, e2):
        a1r, a1i, b1r, b1i = e1
        a2r, a2i, b2r, b2i = e2
        ar, ai = cmul(a2r, a2i, a1r, a1i)
        br, bi = cmul(a2r, a2i, b1r, b1i)
        return ar, ai, br + b2r, bi + b2i

    p_re, p_im, s_re, s_im = lax.associative_scan(combine, (a_b_re, a_b_im, bu_re, bu_im), axis=1)
    i_re, i_im = cmul(p_re, p_im, h0_re.astype(f32)[:, None], h0_im.astype(f32)[:, None])
    h_re = i_re + s_re
    h_im = i_im + s_im
    y = (jnp.einsum('blgp,ghp->blgh', h_re, c_re.astype(f32))
         - jnp.einsum('blgp,ghp->blgh', h_im, c_im.astype(f32))
         + d.astype(f32).reshape(SSM_GROUPS, SSM_GROUP) * uf)
    return y.reshape(bsz, L, W_SSM).astype(u.dtype), h_re[:, -1], h_im[:, -1]


def gated_delta_chunked(q, k, v, g, beta, s0):
    f32 = jnp.float32
    bsz, L, nh, dk = q.shape
    dv = v.shape[-1]
    cs = min(GDN_CHUNK, L)
    n_c = -(-L // cs)
    pad = n_c * cs - L

    def blocks(t):
        t = jnp.pad(t.astype(f32), [(0, 0), (0, pad)] + [(0, 0)] * (t.ndim - 2))
        t = t.reshape((bsz, n_c, cs) + t.shape[2:])
        return jnp.moveaxis(jnp.moveaxis(t, 1, 0), 2, 3)

    qb, kb, vb, gb, bb = blocks(q), blocks(k), blocks(v), blocks(g), blocks(beta)
    gc = jnp.cumsum(gb, axis=-1)
    idx = jnp.arange(cs)
    strict = idx[:, None] > idx[None, :]
    incl = idx[:, None] >= idx[None, :]
    diff = gc[..., :, None] - gc[..., None, :]
    dec_strict = jnp.where(strict, jnp.exp(jnp.where(strict, diff, 0.0)), 0.0)
    dec_incl = jnp.where(incl, jnp.exp(jnp.where(incl, diff, 0.0)), 0.0)
    m = bb[..., :, None] * jnp.einsum('nbhid,nbhjd->nbhij', kb, kb) * dec_strict
    egc = jnp.exp(gc)
    rhs = jnp.concatenate([bb[..., None] * vb, (bb * egc)[..., None] * kb], axis=-1)
    sol = lax.linalg.triangular_solve(m + jnp.eye(cs, dtype=f32), rhs, left_side=True, lower=True, unit_diagonal=True)
    u_v, w = sol[..., :dv], sol[..., dv:]
    qk = jnp.einsum('nbhid,nbhjd->nbhij', qb, kb) * dec_incl
    q_dec = qb * egc[..., None]
    g_last = gc[..., -1]
    k_dec = kb * jnp.exp(g_last[..., None] - gc)[..., None]

    def step(S, xs):
        u_v_c, w_c, q_c, qk_c, k_c, gl_c = xs
        u = u_v_c - jnp.einsum('bhik,bhkv->bhiv', w_c, S)
        o = jnp.einsum('bhik,bhkv->bhiv', q_c, S) + jnp.einsum('bhij,bhjv->bhiv', qk_c, u)
        S = jnp.exp(gl_c)[..., None, None] * S + jnp.einsum('bhik,bhiv->bhkv', k_c, u)
        return S, o

    s_fin, o = lax.scan(step, s0.astype(f32), (u_v, w, q_dec, qk, k_dec, g_last))
    o = jnp.moveaxis(jnp.moveaxis(o, 3, 2), 0, 1).reshape(bsz, n_c * cs, nh, dv)[:, :L]
    return o, s_fin


def stick_breaking(q, k, v, q_pos, k_pos):
    f32 = jnp.float32
    z = jnp.einsum('bqhd,bkhd->bhqk', q.astype(f32), k.astype(f32)) * (q.shape[-1] ** -0.5)
    causal = k_pos[None, :] < q_pos[:, None]
    log_neg = jnp.where(causal, jax.nn.log_sigmoid(-z), 0.0)
    csum = jnp.cumsum(log_neg, axis=-1)
    log_w = jax.nn.log_sigmoid(z) + csum[..., -1:] - csum
    wts = jnp.where(causal, jnp.exp(log_w), 0.0)
    return jnp.einsum('bhqk,bkhd->bqhd', wts, v.astype(f32)).astype(v.dtype)


def sb_prompt(q, k, v):
    bsz, L, nh, dh = q.shape
    nb = L // SB_BLOCK
    qb = jnp.moveaxis(q.reshape(bsz, nb, SB_BLOCK, nh, dh), 1, 0)
    k_pos = jnp.arange(L)

    def blk(args):
        qi, i = args
        q_pos = i * SB_BLOCK + jnp.arange(SB_BLOCK)
        return stick_breaking(qi, k, v, q_pos, k_pos)

    o = lax.map(blk, (qb, jnp.arange(nb)))
    return jnp.moveaxis(o, 0, 1).reshape(bsz, L, nh, dh)


def sb_sample(q, k, v, k_past, v_past):
    past = k_past.shape[1]
    t_new = q.shape[1]
    kk = jnp.concatenate([k_past.astype(k.dtype), k], axis=1)
    vv = jnp.concatenate([v_past.astype(v.dtype), v], axis=1)
    q_pos = past + jnp.arange(t_new)
    k_pos = jnp.arange(past + t_new)
    return stick_breaking(q, kk, vv, q_pos, k_pos)


def decoder_layer(x, c, p, st, attend):
    f32 = jnp.float32
    h0_re, h0_im, convb_buf, s0, convd_buf = st
    bsz, L, _ = x.shape
    mod = (jax.nn.silu(c) @ p['w_ada'] + p['b_ada'])[:, None, :]
    sh1, sc1, g1, sh2, sc2, g2 = jnp.split(mod, 6, axis=-1)
    h = layer_norm(x) * (1.0 + sc1) + sh1

    proj = h @ p['w_in']
    o = 0
    u_ssm = proj[..., o:o + W_SSM]; o += W_SSM
    b_g = proj[..., o:o + W_CONV]; o += W_CONV
    c_g = proj[..., o:o + W_CONV]; o += W_CONV
    x_c = proj[..., o:o + W_CONV]; o += W_CONV
    qkv_g = proj[..., o:o + W_GDN_QKV]; o += W_GDN_QKV
    z_g = proj[..., o:o + GDN_H * GDN_DV]; o += GDN_H * GDN_DV
    beta_logit = proj[..., o:o + GDN_H]; o += GDN_H
    a_logit = proj[..., o:o + GDN_H]; o += GDN_H
    q_sb = proj[..., o:o + SB_H * SB_DH].reshape(bsz, L, SB_H, SB_DH); o += SB_H * SB_DH
    k_sb = proj[..., o:o + SB_H * SB_DH].reshape(bsz, L, SB_H, SB_DH); o += SB_H * SB_DH
    v_sb = proj[..., o:o + SB_H * SB_DH].reshape(bsz, L, SB_H, SB_DH)

    y_s, h_re, h_im = s5_ssm(u_ssm, p['ssm_a_re'], p['ssm_a_im'], p['ssm_log_dt'], p['ssm_b_re'], p['ssm_b_im'],
                             p['ssm_c_re'], p['ssm_c_im'], p['ssm_d'], h0_re, h0_im)
    y_s = jax.nn.gelu(y_s)
    y_a = y_s * jax.nn.sigmoid(y_s @ p['w_glu'])

    conv_out, convb_new = causal_conv(c_g * x_c, p['conv_b_w'], convb_buf)
    y_b = b_g * conv_out

    qkv_c, convd_new = causal_conv(qkv_g, p['delta_conv_w'], convd_buf)
    qkv_c = jax.nn.silu(qkv_c.astype(f32))
    qd = l2norm(qkv_c[..., :GDN_H * GDN_DK].reshape(bsz, L, GDN_H, GDN_DK)) * (GDN_DK ** -0.5)
    kd = l2norm(qkv_c[..., GDN_H * GDN_DK:2 * GDN_H * GDN_DK].reshape(bsz, L, GDN_H, GDN_DK))
    vd = qkv_c[..., 2 * GDN_H * GDN_DK:].reshape(bsz, L, GDN_H, GDN_DV)
    beta_d = jax.nn.sigmoid(beta_logit.astype(f32))
    g_d = -jnp.exp(p['delta_a_log'].astype(f32)) * jax.nn.softplus(a_logit.astype(f32) + p['delta_dt_bias'].astype(f32))
    o_d, s_new = gated_delta_chunked(qd, kd, vd, g_d, beta_d, s0)
    o_d = (o_d * lax.rsqrt(jnp.mean(jnp.square(o_d), axis=-1, keepdims=True) + RMS_EPS)
           * p['delta_norm_w'].astype(f32) * jax.nn.silu(z_g.astype(f32).reshape(bsz, L, GDN_H, GDN_DV)))
    y_c = o_d.reshape(bsz, L, W_MIX).astype(x.dtype)

    y_d = attend(q_sb, k_sb, v_sb).reshape(bsz, L, W_MIX)

    branches = jnp.stack([y_a, y_b, y_c, y_d], axis=2)
    br = jnp.einsum('blnw,nwd->blnd', branches, p['w_branch'])
    gates = jax.nn.sigmoid(h @ p['w_gate']).reshape(bsz, L, N_BRANCH, D_MODEL)
    mixed = jnp.sum(gates * br, axis=2) @ p['w_o']
    x = layer_norm(ALPHA_DN * x + (1.0 + g1) * mixed, p['ln1_g'], p['ln1_b'])

    h2 = layer_norm(x) * (1.0 + sc2) + sh2
    up_a, up_b = jnp.split(h2 @ p['w_ffn_up'], 2, axis=-1)
    ffn = (jax.nn.silu(up_a) * up_b) @ p['w_ffn_down']
    x = layer_norm(ALPHA_DN * x + (1.0 + g2) * ffn, p['ln2_g'], p['ln2_b'])
    return x, (k_sb, v_sb, h_re, h_im, convb_new, s_new, convd_new)


def setup_inputs(seed: int = 0) -> dict:
    key = jax.random.key(seed)
    ks = iter(jax.random.split(key, 48))
    nxt = lambda: next(ks)
    f32 = jnp.float32
    n_pages = PAST_LEN // PAGE_SIZE
    n_used = DEC_BATCH * n_pages
    n_pool = n_used + max(1, n_used // 4)
    nrm = lambda shape, s: jax.random.normal(nxt(), shape, f32) * s

    x_prompt = nrm((BATCH, SEQ, D_MODEL), 1.0)
    x_sample = nrm((DEC_BATCH, DEC_SEQ, D_MODEL), 1.0)
    cache_k = nrm((DEPTH, n_pool, PAGE_SIZE, SB_H, SB_DH), 1.0)
    cache_v = nrm((DEPTH, n_pool, PAGE_SIZE, SB_H, SB_DH), 1.0)
    state_ssm_re = nrm((DEPTH, DEC_BATCH, SSM_GROUPS, SSM_P), 0.5)
    state_ssm_im = nrm((DEPTH, DEC_BATCH, SSM_GROUPS, SSM_P), 0.5)
    state_conv_b = nrm((DEPTH, DEC_BATCH, CONV_B_W - 1, W_CONV), 1.0)
    state_delta = nrm((DEPTH, DEC_BATCH, GDN_H, GDN_DK, GDN_DV), 0.1)
    state_conv_delta = nrm((DEPTH, DEC_BATCH, GDN_CONV - 1, W_GDN_QKV), 1.0)
    page_table = jax.random.permutation(nxt(), n_pool)[:n_used].reshape(DEC_BATCH, n_pages).astype(jnp.int32)
    c_prompt = nrm((BATCH, D_MODEL), 1.0)
    c_sample = nrm((DEC_BATCH, D_MODEL), 1.0)

    w_ada = nrm((DEPTH, D_MODEL, 6 * D_MODEL), 0.2 * D_MODEL ** -0.5)
    b_ada = nrm((DEPTH, 6 * D_MODEL), 0.01)
    w_in = nrm((DEPTH, D_MODEL, N_IN), D_MODEL ** -0.5)
    ssm_a_re = -0.5 + nrm((DEPTH, SSM_GROUPS, SSM_P), 0.01)
    ssm_a_im = jnp.pi * jnp.arange(SSM_P, dtype=f32) + nrm((DEPTH, SSM_GROUPS, SSM_P), 0.01)
    ssm_log_dt = jax.random.uniform(nxt(), (DEPTH, SSM_GROUPS), f32, math.log(1e-3), math.log(1e-1))
    ssm_b_re = nrm((DEPTH, SSM_GROUPS, SSM_P, SSM_GROUP), (2 * SSM_GROUP) ** -0.5)
    ssm_b_im = nrm((DEPTH, SSM_GROUPS, SSM_P, SSM_GROUP), (2 * SSM_GROUP) ** -0.5)
    ssm_c_re = nrm((DEPTH, SSM_GROUPS, SSM_GROUP, SSM_P), (2 * SSM_P) ** -0.5)
    ssm_c_im = nrm((DEPTH, SSM_GROUPS, SSM_GROUP, SSM_P), (2 * SSM_P) ** -0.5)
    ssm_d = nrm((DEPTH, W_SSM), 1.0)
    w_glu = nrm((DEPTH, W_SSM, W_SSM), W_SSM ** -0.5)
    conv_b_w = nrm((DEPTH, CONV_B_W, W_CONV), CONV_B_W ** -0.5)
    delta_conv_w = nrm((DEPTH, GDN_CONV, W_GDN_QKV), GDN_CONV ** -0.5)
    delta_a_log = jnp.log(jax.random.uniform(nxt(), (DEPTH, GDN_H), f32, 1.0, 16.0))
    dt0 = jnp.exp(jax.random.uniform(nxt(), (DEPTH, GDN_H), f32, math.log(1e-3), math.log(1e-1)))
    delta_dt_bias = jnp.log(jnp.expm1(dt0))
    delta_norm_w = 1.0 + nrm((DEPTH, GDN_DV), 0.01)
    w_branch = nrm((DEPTH, N_BRANCH, W_MIX, D_MODEL), W_MIX ** -0.5)
    w_gate = nrm((DEPTH, D_MODEL, N_BRANCH * D_MODEL), D_MODEL ** -0.5)
    w_o = nrm((DEPTH, D_MODEL, D_MODEL), BETA_DN * D_MODEL ** -0.5)
    ln1_g = 1.0 + nrm((DEPTH, D_MODEL), 0.01)
    ln1_b = nrm((DEPTH, D_MODEL), 0.01)
    w_ffn_up = nrm((DEPTH, D_MODEL, 2 * D_FF), D_MODEL ** -0.5)
    w_ffn_down = nrm((DEPTH, D_FF, D_MODEL), BETA_DN * D_FF ** -0.5)
    ln2_g = 1.0 + nrm((DEPTH, D_MODEL), 0.01)
    ln2_b = nrm((DEPTH, D_MODEL), 0.01)
    return {'x_prompt': x_prompt, 'x_sample': x_sample, 'cache_k': cache_k, 'cache_v': cache_v,
            'state_ssm_re': state_ssm_re, 'state_ssm_im': state_ssm_im, 'state_conv_b': state_conv_b,
            'state_delta': state_delta, 'state_conv_delta': state_conv_delta, 'page_table': page_table,
            'c_prompt': c_prompt, 'c_sample': c_sample, 'w_ada': w_ada, 'b_ada': b_ada, 'w_in': w_in,
            'ssm_a_re': ssm_a_re, 'ssm_a_im': ssm_a_im, 'ssm_log_dt': ssm_log_dt, 'ssm_b_re': ssm_b_re,
            'ssm_b_im': ssm_b_im, 'ssm_c_re': ssm_c_re, 'ssm_c_im': ssm_c_im, 'ssm_d': ssm_d, 'w_glu': w_glu,
            'conv_b_w': conv_b_w, 'delta_conv_w': delta_conv_w, 'delta_a_log': delta_a_log,
            'delta_dt_bias': delta_dt_bias, 'delta_norm_w': delta_norm_w, 'w_branch': w_branch, 'w_gate': w_gate,
            'w_o': w_o, 'ln1_g': ln1_g, 'ln1_b': ln1_b, 'w_ffn_up': w_ffn_up, 'w_ffn_down': w_ffn_down,
            'ln2_g': ln2_g, 'ln2_b': ln2_b}


def reference(x_prompt, x_sample, cache_k, cache_v, state_ssm_re, state_ssm_im, state_conv_b, state_delta,
              state_conv_delta, page_table, c_prompt, c_sample, w_ada, b_ada, w_in, ssm_a_re, ssm_a_im, ssm_log_dt,
              ssm_b_re, ssm_b_im, ssm_c_re, ssm_c_im, ssm_d, w_glu, conv_b_w, delta_conv_w, delta_a_log,
              delta_dt_bias, delta_norm_w, w_branch, w_gate, w_o, ln1_g, ln1_b, w_ffn_up, w_ffn_down, ln2_g, ln2_b):
    bp = x_prompt.shape[0]
    bs = x_sample.shape[0]
    dt_p = x_prompt.dtype
    x_p, x_s = x_prompt, x_sample
    prompt_new, sample_new = [], []
    for l in range(DEPTH):
        p = {'w_ada': w_ada[l], 'b_ada': b_ada[l], 'w_in': w_in[l], 'ssm_a_re': ssm_a_re[l], 'ssm_a_im': ssm_a_im[l],
             'ssm_log_dt': ssm_log_dt[l], 'ssm_b_re': ssm_b_re[l], 'ssm_b_im': ssm_b_im[l], 'ssm_c_re': ssm_c_re[l],
             'ssm_c_im': ssm_c_im[l], 'ssm_d': ssm_d[l], 'w_glu': w_glu[l], 'conv_b_w': conv_b_w[l],
             'delta_conv_w': delta_conv_w[l], 'delta_a_log': delta_a_log[l], 'delta_dt_bias': delta_dt_bias[l],
             'delta_norm_w': delta_norm_w[l], 'w_branch': w_branch[l], 'w_gate': w_gate[l], 'w_o': w_o[l],
             'ln1_g': ln1_g[l], 'ln1_b': ln1_b[l], 'w_ffn_up': w_ffn_up[l], 'w_ffn_down': w_ffn_down[l],
             'ln2_g': ln2_g[l], 'ln2_b': ln2_b[l]}
        st_p = (jnp.zeros((bp, SSM_GROUPS, SSM_P), dt_p), jnp.zeros((bp, SSM_GROUPS, SSM_P), dt_p),
                jnp.zeros((bp, CONV_B_W - 1, W_CONV), dt_p), jnp.zeros((bp, GDN_H, GDN_DK, GDN_DV), dt_p),
                jnp.zeros((bp, GDN_CONV - 1, W_GDN_QKV), dt_p))
        x_p, new_p = decoder_layer(x_p, c_prompt, p, st_p, sb_prompt)
        prompt_new.append(new_p)
        k_past = cache_k[l][page_table].reshape(bs, -1, SB_H, SB_DH)
        v_past = cache_v[l][page_table].reshape(bs, -1, SB_H, SB_DH)
        attend_s = functools.partial(sb_sample, k_past=k_past, v_past=v_past)
        st_s = (state_ssm_re[l], state_ssm_im[l], state_conv_b[l], state_delta[l], state_conv_delta[l])
        x_s, new_s = decoder_layer(x_s, c_sample, p, st_s, attend_s)
        sample_new.append(new_s)
    k_p, v_p, re_p, im_p, cb_p, d_p, cd_p = [jnp.stack(t, axis=0) for t in zip(*prompt_new)]
    k_s, v_s, re_s, im_s, cb_s, d_s, cd_s = [jnp.stack(t, axis=0) for t in zip(*sample_new)]
    return (x_p, x_s, k_p, v_p, k_s, v_s, re_p, im_p, re_s, im_s, cb_p, cb_s, d_p, d_s, cd_p, cd_s)
```

```python
import numpy as np
from contextlib import ExitStack
import concourse.bass as bass
import concourse.mybir as mybir
from concourse.bass_utils import run_bass_kernel_spmd

F32 = mybir.dt.float32
BF16 = mybir.dt.bfloat16
I32 = mybir.dt.int32
AF = mybir.ActivationFunctionType
ALU = mybir.AluOpType

D = 1024
L = 8192
DEPTH = 2
NS = 16
TS = 4
N_IN = 2824
D_FF = 2816
ALPHA = (2 * DEPTH) ** 0.25
NT = 256
C_U, C_B, C_C, C_X, C_QKV, C_Z, C_BETA, C_A, C_Q, C_K, C_V = 0, 256, 512, 768, 1024, 1792, 2048, 2052, 2056, 2312, 2568


class Sched:
    def __init__(self, nc, es, ndma=16):
        self.nc = nc
        self.eng = {'pe': nc.tensor, 'act': nc.scalar, 'dve': nc.vector, 'pool': nc.gpsimd, 'sp': nc.sync}
        self.sem = {k: es.enter_context(nc.semaphore('s_' + k)) for k in ('pe', 'act', 'dve', 'pool')}
        self.cnt = {k: 0 for k in self.sem}
        self.dsem = {q: [es.enter_context(nc.semaphore('d%s%d' % (q, i))) for i in range(ndma)] for q in ('sp', 'pool')}
        self.duse = {q: [0] * ndma for q in ('sp', 'pool')}
        self.dnext = {'sp': 0, 'pool': 0}
        self.waited = {k: {} for k in self.eng}
        self.lastw = {}
        self.readers = {}
        self.out_tokens = []

    def _semof(self, key):
        return self.sem[key] if isinstance(key, str) else self.dsem[key[1]][key[2]]

    def _wait(self, eng, deps):
        best = {}
        for k, v in deps:
            if eng == 'pe' and k == 'pe':
                continue
            if k == eng and (k, v) not in getattr(self, '_raw', ()):
                continue
            if best.get(k, 0) < v:
                best[k] = v
        w = self.waited[eng]
        for k, v in best.items():
            if w.get(k, 0) >= v:
                continue
            self.eng[eng].wait_ge(self._semof(k), v)
            w[k] = v

    def _deps(self, reads, writes):
        deps = set()
        self._raw = set()
        for b in reads:
            if b in self.lastw:
                deps.add(self.lastw[b])
                self._raw.add(self.lastw[b])
        for b in writes:
            if b in self.lastw:
                deps.add(self.lastw[b])
            deps |= self.readers.get(b, set())
        return deps

    def _record(self, tok, reads, writes):
        for b in reads:
            self.readers.setdefault(b, set()).add(tok)
        for b in writes:
            self.lastw[b] = tok
            self.readers[b] = set()

    def op(self, eng, fn, reads=(), writes=()):
        self._wait(eng, self._deps(reads, writes))
        ins = fn(self.eng[eng])
        self.cnt[eng] += 1
        ins.then_inc(self.sem[eng], 1)
        self._record((eng, self.cnt[eng]), reads, writes)

    def dma(self, q, out, in_, reads=(), writes=(), is_output=False):
        slot = self.dnext[q]
        self.dnext[q] = (slot + 1) % len(self.dsem[q])
        deps = self._deps(reads, writes)
        if self.duse[q][slot] > 0:
            deps.add((('d', q, slot), 16 * self.duse[q][slot]))
        self._wait(q, deps)
        self.eng[q].dma_start(out=out, in_=in_).then_inc(self.dsem[q][slot], 16)
        self.duse[q][slot] += 1
        tok = (('d', q, slot), 16 * self.duse[q][slot])
        self._record(tok, reads, writes)
        if is_output:
            self.out_tokens.append(tok)
        return tok

    def dma_custom(self, q, fn, reads=(), writes=()):
        slot = self.dnext[q]
        self.dnext[q] = (slot + 1) % len(self.dsem[q])
        deps = self._deps(reads, writes)
        if self.duse[q][slot] > 0:
            deps.add((('d', q, slot), 16 * self.duse[q][slot]))
        self._wait(q, deps)
        fn(self.eng[q]).then_inc(self.dsem[q][slot], 16)
        self.duse[q][slot] += 1
        tok = (('d', q, slot), 16 * self.duse[q][slot])
        self._record(tok, reads, writes)
        return tok

    def finish(self):
        deps = set()
        for q in ('sp', 'pool'):
            for i, u in enumerate(self.duse[q]):
                if u:
                    deps.add((('d', q, i), 16 * u))
        self._wait('sp', deps)


import os
WITH_A = os.environ.get('WITH_A', '1') == '1'
DSTOP = int(os.environ.get('DSTOP', '99'))
DV = int(os.environ.get('DV', '3'))
WITH_C = os.environ.get('WITH_C', '1') == '1'
WITH_D = os.environ.get('WITH_D', '1') == '1'
import os
N_LAYERS = DEPTH
STG = set(os.environ.get('STAGES', 'G,cast,inproj,kv,conv,merge,ffn,xout').split(','))
NTILES = int(os.environ.get('NTILES', '32'))
N_LAYERS = int(os.environ.get('NLAYERS', '2'))


def build_program(n_pool):
    nc = bass.Bass("TRN2", target_bir_lowering=False)
    dram = lambda n, sh, dt=F32, kind="ExternalInput": nc.dram_tensor(n, sh, dt, kind=kind).ap()
    xp = dram("xp", [L, D])
    xs = dram("xs", [NS * TS, D])
    cTp = dram("cTp", [128, 8, 1])
    cTs = dram("cTs", [128, 8, NS])
    w_ada = dram("w_ada", [DEPTH, D, 6 * D])
    b_adaT = dram("b_adaT", [DEPTH, 128, 48])
    w_in = dram("w_in", [DEPTH, D, N_IN])
    w_gate = dram("w_gate", [DEPTH, D, 4 * D])
    w_branch = dram("w_branch", [DEPTH, 4 * 256, D])
    w_o = dram("w_o", [DEPTH, D, D])
    w_up = dram("w_up", [DEPTH, D, 2 * D_FF])
    w_down = dram("w_down", [DEPTH, D_FF, D])
    lnp = dram("lnp", [DEPTH, 128, 4, D])
    convbw = dram("convbw", [DEPTH, 128, 2, 3])
    ssmw = dram("ssmw", [DEPTH, 128, 8, 67])
    ssmd = dram("ssmd", [DEPTH, 128, 2])
    st_ssm = dram("st_ssm", [DEPTH, 128, 8, 2, NS])
    w_glu = dram("w_glu", [DEPTH, 256, 256])
    dcw = dram("dcw", [DEPTH, 128, 6, 4])
    dpar = dram("dpar", [DEPTH, 128, 8 + 64])
    dnw = dram("dnw", [DEPTH, 128, 1])
    st_dl = dram("st_dl", [DEPTH, NS, 4, 64, 64])
    ck = dram("ck", [DEPTH * n_pool * 128, 256])
    cv = dram("cv", [DEPTH * n_pool * 128, 256])
    ptab = dram("ptab", [128, NS * 16], I32)
    st_cb = dram("st_cb", [DEPTH, 128, 2, NS, 2])
    st_cd = dram("st_cd", [DEPTH, 128, 6, NS, 3])
    OUT = "ExternalOutput"
    o_yp = dram("o_yp", [L, D], kind=OUT)
    o_ys = dram("o_ys", [NS * TS, D], kind=OUT)
    o_kp = dram("o_kp", [DEPTH, L, 256], kind=OUT)
    o_vp = dram("o_vp", [DEPTH, L, 256], kind=OUT)
    o_ks = dram("o_ks", [DEPTH, NS * TS, 256], kind=OUT)
    o_vs = dram("o_vs", [DEPTH, NS * TS, 256], kind=OUT)
    o_ssp = dram("o_ssp", [DEPTH, 128, 8, 2, 1], kind=OUT)
    o_sss = dram("o_sss", [DEPTH, 128, 8, 2, NS], kind=OUT)
    o_dlp = dram("o_dlp", [DEPTH, 1, 4, 64, 64], kind=OUT)
    o_dls = dram("o_dls", [DEPTH, NS, 4, 64, 64], kind=OUT)
    o_cbp = dram("o_cbp", [DEPTH, 128, 2, 2], kind=OUT)
    o_cbs = dram("o_cbs", [DEPTH, 128, 2, NS, 2], kind=OUT)
    o_cdp = dram("o_cdp", [DEPTH, 128, 6, 3], kind=OUT)
    o_cds = dram("o_cds", [DEPTH, 128, 6, NS, 3], kind=OUT)
    INT = "Internal"
    wada_bf = dram("wada_bf", [DEPTH, D, 6 * D], BF16, kind=INT)
    win_bf = dram("win_bf", [DEPTH, D, N_IN], BF16, kind=INT)
    wgate_bf = dram("wgate_bf", [DEPTH, D, 4 * D], BF16, kind=INT)
    wbr_bf = dram("wbr_bf", [DEPTH, 4 * 256, D], BF16, kind=INT)
    wo_bf = dram("wo_bf", [DEPTH, D, D], BF16, kind=INT)
    wglu_bf = dram("wglu_bf", [DEPTH, 256, 256], BF16, kind=INT)
    wup_bf = dram("wup_bf", [DEPTH, D, 2 * D_FF], BF16, kind=INT)
    wdown_bf = dram("wdown_bf", [DEPTH, D_FF, D], BF16, kind=INT)
    xmid_p = dram("xmid_p", [L, D], kind=INT)
    xmid_s = dram("xmid_s", [NS * TS, D], kind=INT)
    KT_d = dram("KT_d", [2, 128, L], BF16, kind=INT)
    V_d = dram("V_d", [L, 256], BF16, kind=INT)

    with ExitStack() as es:
        S = Sched(nc, es)
        sb = lambda n, sh, dt=F32: es.enter_context(nc.sbuf_tensor(n, sh, dt))
        OP = S.op
        identf = sb("identf", [128, 128])
        ident = sb("ident", [128, 128], BF16)
        OP('pool', lambda e: e.memset(identf[:], 0.0), writes=['identf'])
        OP('pool', lambda e: e.affine_select(out=identf[:], in_=identf[:], pattern=[[-1, 128]],
                                             compare_op=ALU.not_equal, fill=1.0, base=0, channel_multiplier=1),
           reads=['identf'], writes=['identf'])
        OP('dve', lambda e: e.tensor_copy(out=ident[:], in_=identf[:]), reads=['identf'], writes=['ident'])

        wids = {}

        def cast_w(dst, src, rows, name):
            step = 128
            for l in range(DEPTH):
                wids[(name, l)] = []
                for r0 in range(0, rows, step):
                    wid = '%s_%d_%d' % (name, l, r0)
                    wids[(name, l)].append(wid)
                    S.dma('pool', dst[l, r0:r0 + step, :], src[l, r0:r0 + step, :], writes=[wid])
        cast_w(wada_bf, w_ada, D, 'wada_bf')
        cast_w(wglu_bf, w_glu, 256, 'wglu_bf')
        cast_w(win_bf, w_in, D, 'win_bf')
        if 'cast' in STG:
          cast_w(wgate_bf, w_gate, D, 'wgate_bf')
          cast_w(wbr_bf, w_branch, 4 * 256, 'wbr_bf')
          cast_w(wo_bf, w_o, D, 'wo_bf')
          cast_w(wup_bf, w_up, D, 'wup_bf')
          cast_w(wdown_bf, w_down, D_FF, 'wdown_bf')

        psum = [es.enter_context(nc.psum_tensor("ps%d" % i, [128, 512], F32)) for i in range(8)]
        psum_bf = [p[:].bitcast(BF16) for p in psum]
        PID = ['ps%d' % i for i in range(8)]

        NW = 3
        wbuf = [sb("wbuf%d" % i, [128, 8, 512], BF16) for i in range(NW)]
        wstate = {'i': 0}

        def load_w(wname, l, src2d, c0, ncols, kts=8, k0=0):
            i = wstate['i']; wstate['i'] = (i + 1) % NW
            src = src2d[k0 * 128:(k0 + kts) * 128, c0:c0 + ncols].rearrange("(kt p) n -> p kt n", p=128)
            S.dma('sp', wbuf[i][:, 0:kts, 0:ncols], src, reads=wids[(wname, l)], writes=['wbuf%d' % i])
            return wbuf[i], 'wbuf%d' % i

        scT_p = sb("scT_p", [128, 8, 1]); scT_s = sb("scT_s", [128, 8, NS])
        cbf_p = sb("cbf_p", [128, 8, 1], BF16); cbf_s = sb("cbf_s", [128, 8, NS], BF16)
        S.dma('sp', scT_p[:], cTp, writes=['scT_p'])
        S.dma('sp', scT_s[:], cTs, writes=['scT_s'])
        OP('act', lambda e: e.activation(out=cbf_p[:], in_=scT_p[:], func=AF.Silu), reads=['scT_p'], writes=['cbf_p'])
        OP('act', lambda e: e.activation(out=cbf_s[:], in_=scT_s[:], func=AF.Silu), reads=['scT_s'], writes=['cbf_s'])
        modT_p = sb("modT_p", [128, 48, 1]); modT_s = sb("modT_s", [128, 48, NS])
        badaT = sb("badaT", [128, 48])
        Gp = sb("Gp", [128, 2, D]); Gs = Gp
        LNP = sb("LNP", [128, 2, D])
        cbw = sb("cbw", [128, 2, 3])
        gsig = sb("gsig", [128, NT]); gtmp = sb("gtmp", [128, NT])

        def ada(l):
            S.dma('sp', badaT[:], b_adaT[l], writes=['badaT'])
            S.dma('sp', cbw[:], convbw[l], writes=['cbw'])
            for g in range(12):
                wb, wid = load_w('wada_bf', l, wada_bf[l], g * 512, 512)
                for m in range(4):
                    j = g * 4 + m
                    ps = psum[j % 2]; pid = PID[j % 2]
                    for kt in range(8):
                        OP('pe', lambda e, kt=kt: e.matmul(ps[:, 0:1], lhsT=wb[:, kt, m * 128:(m + 1) * 128], rhs=cbf_p[:, kt, :],
                                                          start=(kt == 0), stop=(kt == 7)), reads=[wid, 'cbf_p'], writes=[pid])
                    for kt in range(8):
                        OP('pe', lambda e, kt=kt: e.matmul(ps[:, 16:16 + NS], lhsT=wb[:, kt, m * 128:(m + 1) * 128], rhs=cbf_s[:, kt, :],
                                                          start=(kt == 0), stop=(kt == 7)), reads=[wid, 'cbf_s'], writes=[pid])
                    add1 = 1.0 if (j // 8) in (1, 2, 4, 5) else 0.0
                    OP('dve', lambda e: e.tensor_scalar(out=modT_p[:, j, :], in0=ps[:, 0:1], scalar1=badaT[:, j:j + 1], scalar2=add1,
                                                        op0=ALU.add, op1=ALU.add), reads=[pid, 'badaT'], writes=['modT_p'])
                    OP('dve', lambda e: e.tensor_scalar(out=modT_s[:, j, :], in0=ps[:, 16:16 + NS], scalar1=badaT[:, j:j + 1], scalar2=add1,
                                                        op0=ALU.add, op1=ALU.add), reads=[pid, 'badaT'], writes=['modT_s'])

        def make_G(sample):
            for gi, off in ((0, 16), (1, 40)):
                for half in range(2):
                    ps = psum[2 + half]; pid = PID[2 + half]
                    for k4 in range(4):
                        kt = half * 4 + k4
                        if not sample:
                            OP('dve', lambda e: e.tensor_copy(out=gtmp[:, 0:128], in_=modT_p[:, off + kt, 0:1].to_broadcast([128, 128])),
                               reads=['modT_p'], writes=['gtmp'])
                            OP('pe', lambda e: e.matmul(ps[:, k4 * 128:(k4 + 1) * 128], lhsT=gtmp[:, 0:128], rhs=identf[:],
                                                        start=True, stop=True), reads=['gtmp', 'identf'], writes=[pid])
                        else:
                            OP('dve', lambda e: e.tensor_copy(out=gtmp[:, 0:64].rearrange("p (s t) -> p s t", t=TS),
                                                              in_=modT_s[:, off + kt, :].unsqueeze(2).to_broadcast([128, NS, TS])),
                               reads=['modT_s'], writes=['gtmp'])
                            OP('pe', lambda e: e.matmul(ps[:64, k4 * 128:(k4 + 1) * 128], lhsT=gtmp[:, 0:64], rhs=identf[:],
                                                        start=True, stop=True), reads=['gtmp', 'identf'], writes=[pid])
                    npp = 64 if sample else 128
                    OP('act', lambda e: e.copy(out=Gp[:npp, gi, half * 512:(half + 1) * 512], in_=ps[:npp, :]), reads=[pid], writes=['G'])

        xt = sb("xt", [128, 2, D])
        tt = sb("tt", [128, D])
        xn = sb("xn", [128, 2, D], BF16)
        stats = sb("stats", [128, 2, 6]); mv = sb("mv", [128, 2]); rstd = sb("rstd", [128, 1])
        hT = sb("hT", [128, 8, NT], BF16)
        kv_tm = sb("kv_tm", [128, 2, 512])
        uT_bf = sb("uT_bf", [128, 2, NT], BF16); uT_f = sb("uT_f", [128, 2, NT])
        bT = sb("bT", [128, 2, NT]); cxT = sb("cxT", [128, 4, NT])
        zT = sb("zT", [128, 2, NT])
        qT = sb("qT", [128, 2, NT], BF16); kT = sb("kT", [128, 2, NT], BF16)
        sTp = sb("sTp", [128, 2, 1, 2 + NT]); sTs = sb("sTs", [128, 2, NS, 2 + TS])
        qkvp = sb("qkvp", [128, 6, 1, 3 + NT]); qkvs = sb("qkvs", [128, 6, NS, 3 + TS])
        cacc = sb("cacc", [128, NT])
        ybT = sb("ybT", [128, 2, NT], BF16)
        mixT = sb("mixT", [128, 8, NT]); mixbf = sb("mixbf", [128, 8, NT], BF16)
        wbr = sb("wbr", [128, 4, 512], BF16)
        actT = sb("actT", [128, 22, NT], BF16)

        onesf = sb("onesf", [128, 256]); onesb = sb("onesb", [128, 128], BF16)
        trisf = gtmp[:, 0:128]; trisb = sb("trisb", [128, 128], BF16)
        amask = mixT[:, 0:2, :]; amaskb = sb("amaskb", [128, 2, NT], BF16)
        OP('pool', lambda e: e.memset(onesf[:], 1.0), writes=['onesf'])
        OP('dve', lambda e: e.tensor_copy(out=onesb[:], in_=onesf[:, 0:128]), reads=['onesf'], writes=['onesb'])
        OP('pool', lambda e: e.affine_select(out=trisf[:], in_=onesf[:, 0:128], pattern=[[-1, 128]], compare_op=ALU.is_gt, fill=0.0,
                                             base=0, channel_multiplier=1), reads=['onesf'], writes=['gtmp'])
        OP('dve', lambda e: e.tensor_copy(out=trisb[:], in_=trisf[:]), reads=['gtmp'], writes=['trisb'])
        for r in range(2):
            OP('pool', lambda e: e.affine_select(out=amask[:, r, :], in_=onesf[:, 0:NT], pattern=[[1, NT]], compare_op=ALU.is_gt, fill=0.0,
                                                 base=-128 * r, channel_multiplier=-1), reads=['onesf'], writes=['mixT'])
        OP('dve', lambda e: e.tensor_copy(out=amaskb[:], in_=amask[:]), reads=['mixT'], writes=['amaskb'])
        vbf = sb("vbf", [128, 2, 256], BF16)
        kch = [sb("kch%d" % i, [128, 256], BF16) for i in range(2)]
        vch = [sb("vch%d" % i, [128, 2, 128], BF16) for i in range(2)]
        carry = sb("carry", [128, 2, NT])
        aE = sb("aE", [128, NT]); aL = sb("aL", [128, NT]); aN = sb("aN", [128, NT], BF16)
        aX = sb("aX", [128, NT]); aW = sb("aW", [128, NT], BF16)
        ydT = sb("ydT", [64, 4, NT], BF16)

        TC = 128
        PI = 3.14159265358979
        sw = sb("sw", [128, 8, 67]); sd = sb("sd", [128, 2])
        s_th = sb("s_th", [128, 8]); s_r = sb("s_r", [128, 8]); s_abr = sb("s_abr", [128, 8]); s_abi = sb("s_abi", [128, 8])
        s_t = [sb("s_t%d" % i, [128, 8]) for i in range(6)]
        s_bb = sb("s_bb", [128, 8, 2, 16])
        rr_a = tt; rr_f = xt[:, 0, :]; rr_i = mixT[:].rearrange("p a b -> p (a b)")[:, 0:8 * TC].bitcast(I32)
        cosP = sb("cosP", [128, 8, TC]); sinP = sb("sinP", [128, 8, TC]); rTP = sb("rTP", [128, 8, 1, TC])
        cosS = sb("cosS", [128, 8, TS]); sinS = sb("sinS", [128, 8, TS]); rTS = sb("rTS", [128, 8, NS, TS])
        iot = sb("iot", [128, TC])
        OP('pool', lambda e: e.iota(iot[:], pattern=[[1, TC]], base=0, channel_multiplier=0, allow_small_or_imprecise_dtypes=True), writes=['iot'])
        Bpad = sb("Bpad", [128, 128], BF16)
        Bmat = sb("Bmat", [128, 8, 2, 128], BF16); Cmat = sb("Cmat", [128, 8, 2, 128], BF16)
        wglu_sb = sb("wglu_sb", [128, 2, 256], BF16)
        hstP = sb("hstP", [128, 8, 2, 1]); hstS = sb("hstS", [128, 8, 2, NS])
        hc = sb("hc", [128, 8, 2, NS]); hct = sb("hct", [128, 8, NS])
        e_ = [sb("e_%d" % i, [128, TC]) for i in range(8)]
        hbf = sb("hbf", [128, 2, TC], BF16)
        ysT = sb("ysT", [128, 2, NT]); ysb = sb("ysb", [128, 2, NT], BF16)
        yaT = sb("yaT", [128, 2, NT], BF16)

        def range_reduce(F):
            a = rr_a[:, 0:F]; f = rr_f[:, 0:F]; i_ = rr_i[:, 0:F]
            OP('dve', lambda e: e.tensor_scalar(out=f, in0=a, scalar1=1.0 / (2 * PI), scalar2=None, op0=ALU.mult), reads=['tt'], writes=['xt'])
            OP('dve', lambda e: e.tensor_copy(out=i_, in_=f), reads=['xt'], writes=['mixT'])
            OP('dve', lambda e: e.tensor_copy(out=f, in_=i_), reads=['mixT'], writes=['xt'])
            OP('dve', lambda e: e.scalar_tensor_tensor(out=a, in0=f, scalar=-2 * PI, in1=a, op0=ALU.mult, op1=ALU.add), reads=['xt', 'tt'], writes=['tt'])
            OP('dve', lambda e: e.tensor_scalar(out=f, in0=a, scalar1=PI, scalar2=2 * PI, op0=ALU.is_gt, op1=ALU.mult), reads=['tt'], writes=['xt'])
            OP('dve', lambda e: e.tensor_sub(out=a, in0=a, in1=f), reads=['tt', 'xt'], writes=['tt'])
            OP('dve', lambda e: e.tensor_scalar(out=f, in0=a, scalar1=-PI, scalar2=2 * PI, op0=ALU.is_lt, op1=ALU.mult), reads=['tt'], writes=['xt'])
            OP('dve', lambda e: e.tensor_add(out=a, in0=a, in1=f), reads=['tt', 'xt'], writes=['tt'])

        def sincos(dst_sin, dst_cos, ang_fn, F, ids):
            for dst, shift in ((dst_sin, 0.0), (dst_cos, PI / 2)):
                ang_fn(shift)
                range_reduce(F)
                OP('act', lambda e: e.activation(out=dst, in_=rr_a[:, 0:F], func=AF.Sin), reads=['tt'], writes=ids)

        def ssm_setup(l):
            S.dma('sp', sw[:], ssmw[l], writes=['sw'])
            S.dma('sp', sd[:], ssmd[l], writes=['sd'])
            S.dma('sp', wglu_sb[:], wglu_bf[l].rearrange("(kt p) n -> p kt n", p=128), reads=wids[('wglu_bf', l)], writes=['wglu_sb'])
            S.dma('sp', hstS[:], st_ssm[l], writes=['hstS'])
            OP('pool', lambda e: e.memset(hstP[:], 0.0), writes=['hstP'])
            a_re = sw[:, :, 0]; a_im = sw[:, :, 1]; ldt = sw[:, :, 2]
            t0, t1, t2, t3, t4, t5 = [x[:] for x in s_t]
            R = ['sw', 'ssmtmp']; W = ['ssmtmp']
            OP('act', lambda e: e.activation(out=t0, in_=ldt, func=AF.Exp), reads=R, writes=W)
            OP('dve', lambda e: e.tensor_tensor(out=s_th[:], in0=t0, in1=a_im, op=ALU.mult), reads=R, writes=W)
            OP('dve', lambda e: e.tensor_tensor(out=t1, in0=t0, in1=a_re, op=ALU.mult), reads=R, writes=W)
            OP('act', lambda e: e.activation(out=s_r[:], in_=t1, func=AF.Exp), reads=R, writes=W)
            def ang0(shift):
                OP('dve', lambda e: e.tensor_scalar(out=rr_a[:, 0:8], in0=s_th[:], scalar1=shift, scalar2=None, op0=ALU.add), reads=R, writes=['tt'])
            sincos(t2, t3, ang0, 8, W)
            OP('dve', lambda e: e.tensor_tensor(out=s_abr[:], in0=s_r[:], in1=t3, op=ALU.mult), reads=R, writes=W)
            OP('dve', lambda e: e.tensor_tensor(out=s_abi[:], in0=s_r[:], in1=t2, op=ALU.mult), reads=R, writes=W)
            OP('dve', lambda e: e.tensor_tensor(out=t0, in0=a_re, in1=a_re, op=ALU.mult), reads=R, writes=W)
            OP('dve', lambda e: e.tensor_tensor(out=t1, in0=a_im, in1=a_im, op=ALU.mult), reads=R, writes=W)
            OP('dve', lambda e: e.tensor_add(out=t0, in0=t0, in1=t1), reads=R, writes=W)
            OP('dve', lambda e: e.reciprocal(out=t0, in_=t0), reads=R, writes=W)
            OP('dve', lambda e: e.tensor_scalar(out=t1, in0=s_abr[:], scalar1=-1.0, scalar2=None, op0=ALU.add), reads=R, writes=W)
            OP('dve', lambda e: e.tensor_tensor(out=t2, in0=t1, in1=a_re, op=ALU.mult), reads=R, writes=W)
            OP('dve', lambda e: e.tensor_tensor(out=t3, in0=s_abi[:], in1=a_im, op=ALU.mult), reads=R, writes=W)
            OP('dve', lambda e: e.tensor_add(out=t2, in0=t2, in1=t3), reads=R, writes=W)
            OP('dve', lambda e: e.tensor_tensor(out=t4, in0=t2, in1=t0, op=ALU.mult), reads=R, writes=W)
            OP('dve', lambda e: e.tensor_tensor(out=t2, in0=s_abi[:], in1=a_re, op=ALU.mult), reads=R, writes=W)
            OP('dve', lambda e: e.tensor_tensor(out=t3, in0=t1, in1=a_im, op=ALU.mult), reads=R, writes=W)
            OP('dve', lambda e: e.tensor_sub(out=t2, in0=t2, in1=t3), reads=R, writes=W)
            OP('dve', lambda e: e.tensor_tensor(out=t5, in0=t2, in1=t0, op=ALU.mult), reads=R, writes=W)
            b_re = sw[:, :, 3:19]; b_im = sw[:, :, 19:35]
            fre = t4.unsqueeze(2).to_broadcast([128, 8, 16]); fim = t5.unsqueeze(2).to_broadcast([128, 8, 16])
            tmpb = rr_f[:, 0:128].rearrange("p (a b) -> p a b", b=16)
            OP('dve', lambda e: e.tensor_tensor(out=s_bb[:, :, 0, :], in0=b_re, in1=fre, op=ALU.mult), reads=R, writes=W)
            OP('dve', lambda e: e.tensor_tensor(out=tmpb, in0=b_im, in1=fim, op=ALU.mult), reads=R + ['xt'], writes=['xt'])
            OP('dve', lambda e: e.tensor_sub(out=s_bb[:, :, 0, :], in0=s_bb[:, :, 0, :], in1=tmpb), reads=R + ['xt'], writes=W)
            OP('dve', lambda e: e.tensor_tensor(out=s_bb[:, :, 1, :], in0=b_im, in1=fre, op=ALU.mult), reads=R, writes=W)
            OP('dve', lambda e: e.tensor_tensor(out=tmpb, in0=b_re, in1=fim, op=ALU.mult), reads=R + ['xt'], writes=['xt'])
            OP('dve', lambda e: e.tensor_add(out=s_bb[:, :, 1, :], in0=s_bb[:, :, 1, :], in1=tmpb), reads=R + ['xt'], writes=W)
            OP('pool', lambda e: e.memset(Cmat[:], 0.0), writes=['Cmat'])
            for j in range(8):
                c0 = ((2 * j) * 16) % 128; c1 = ((2 * j + 1) * 16) % 128
                for ri in range(2):
                    OP('pool', lambda e: e.memset(Bpad[:], 0.0), reads=['Bpad'], writes=['Bpad'])
                    OP('pool', lambda e: e.tensor_copy(out=Bpad[0:64, c0:c0 + 16], in_=s_bb[0:64, j, ri, :]), reads=['ssmtmp', 'Bpad'], writes=['Bpad'])
                    OP('pool', lambda e: e.tensor_copy(out=Bpad[64:128, c1:c1 + 16], in_=s_bb[64:128, j, ri, :]), reads=['ssmtmp', 'Bpad'], writes=['Bpad'])
                    pb = psum_bf[2]
                    OP('pe', lambda e: e.transpose(out=pb[:, 0:128], in_=Bpad[:], identity=ident[:]), reads=['Bpad', 'ident'], writes=[PID[2]])
                    OP('act', lambda e: e.copy(out=Bmat[:, j, ri, :], in_=pb[:, 0:128]), reads=[PID[2]], writes=['Bmat'])
                    csrc = sw[:, j, 35 + 16 * ri:51 + 16 * ri]
                    sc_ = 1.0 if ri == 0 else -1.0
                    OP('dve', lambda e: e.tensor_scalar(out=Cmat[0:64, j, ri, c0:c0 + 16], in0=csrc[0:64], scalar1=sc_, scalar2=None, op0=ALU.mult),
                       reads=['sw', 'Cmat'], writes=['Cmat'])
                    OP('dve', lambda e: e.tensor_scalar(out=Cmat[64:128, j, ri, c1:c1 + 16], in0=csrc[64:128], scalar1=sc_, scalar2=None, op0=ALU.mult),
                       reads=['sw', 'Cmat'], writes=['Cmat'])
            for (T_, cosT, sinT, cid) in ((TC, cosP, sinP, 'tabP'), (TS, cosS, sinS, 'tabS')):
                F = 8 * T_
                def angT(shift, T_=T_, F=F):
                    av = rr_a[:, 0:F].rearrange("p (j t) -> p j t", t=T_)
                    OP('dve', lambda e: e.tensor_tensor(out=av, in0=iot[:, 0:T_].unsqueeze(1).to_broadcast([128, 8, T_]),
                                                        in1=s_th[:].unsqueeze(2).to_broadcast([128, 8, T_]), op=ALU.mult),
                       reads=['iot', 'ssmtmp', 'tt'], writes=['tt'])
                    if shift:
                        OP('dve', lambda e: e.tensor_scalar(out=rr_a[:, 0:F], in0=rr_a[:, 0:F], scalar1=shift, scalar2=None, op0=ALU.add),
                           reads=['tt'], writes=['tt'])
                sincos(sinT[:].rearrange("p j t -> p (j t)"), cosT[:].rearrange("p j t -> p (j t)"), angT, F, [cid])
            OP('dve', lambda e: e.tensor_copy(out=rTP[:, :, 0, :], in_=s_r[:].unsqueeze(2).to_broadcast([128, 8, TC])), reads=['ssmtmp'], writes=['tabP'])
            OP('dve', lambda e: e.memset(rTP[:, :, :, 0:1], 0.0), reads=['tabP'], writes=['tabP'])
            for j in range(8):
                OP('dve', lambda e: e.tensor_copy(out=rTS[:, j, :, :], in_=s_r[:, j:j + 1].unsqueeze(2).to_broadcast([128, NS, TS])), reads=['ssmtmp'], writes=['tabS'])
            OP('dve', lambda e: e.memset(rTS[:, :, :, 0:1], 0.0), reads=['tabS'], writes=['tabS'])

        def ssm_chunk(c, col0, nseq, T_, cosT, sinT, rT, tid, hst, hid):
            n = nseq * T_
            sv = lambda ap: ap.rearrange("p (s t) -> p s t", t=T_)
            abr = s_abr[:].unsqueeze(2).to_broadcast([128, 8, nseq]); abi = s_abi[:].unsqueeze(2).to_broadcast([128, 8, nseq])
            hr = hst[:, :, 0, :]; hi = hst[:, :, 1, :]
            OP('dve', lambda e: e.tensor_tensor(out=hc[:, :, 0, 0:nseq], in0=hr, in1=abr, op=ALU.mult), reads=[hid, 'ssmtmp'], writes=['hc'])
            OP('dve', lambda e: e.tensor_tensor(out=hct[:, :, 0:nseq], in0=hi, in1=abi, op=ALU.mult), reads=[hid, 'ssmtmp'], writes=['hct'])
            OP('dve', lambda e: e.tensor_sub(out=hc[:, :, 0, 0:nseq], in0=hc[:, :, 0, 0:nseq], in1=hct[:, :, 0:nseq]), reads=['hc', 'hct'], writes=['hc'])
            OP('dve', lambda e: e.tensor_tensor(out=hc[:, :, 1, 0:nseq], in0=hi, in1=abr, op=ALU.mult), reads=[hid, 'ssmtmp'], writes=['hc'])
            OP('dve', lambda e: e.tensor_tensor(out=hct[:, :, 0:nseq], in0=hr, in1=abi, op=ALU.mult), reads=[hid, 'ssmtmp', 'hc'], writes=['hct'])
            OP('dve', lambda e: e.tensor_add(out=hc[:, :, 1, 0:nseq], in0=hc[:, :, 1, 0:nseq], in1=hct[:, :, 0:nseq]), reads=['hc', 'hct'], writes=['hc'])
            E = [x[:, 0:n] for x in e_]
            cs = (lambda j: cosT[:, j, :]) if nseq == 1 else (lambda j: cosT[:, j, :].unsqueeze(1).to_broadcast([128, nseq, T_]))
            sn = (lambda j: sinT[:, j, :]) if nseq == 1 else (lambda j: sinT[:, j, :].unsqueeze(1).to_broadcast([128, nseq, T_]))
            vw = (lambda ap: ap) if nseq == 1 else sv
            for j in range(8):
                kt = j // 4
                A_ = psum[4]; B_ = psum[5]
                OP('pe', lambda e: e.matmul(A_[:, 0:n], lhsT=Bmat[:, j, 0, :], rhs=uT_bf[:, kt, col0:col0 + n], start=True, stop=True),
                   reads=['Bmat', 'uT_bf'], writes=[PID[4]])
                OP('pe', lambda e: e.matmul(B_[:, 0:n], lhsT=Bmat[:, j, 1, :], rhs=uT_bf[:, kt, col0:col0 + n], start=True, stop=True),
                   reads=['Bmat', 'uT_bf'], writes=[PID[5]])
                OP('dve', lambda e: e.tensor_tensor(out=vw(E[0]), in0=vw(A_[:, 0:n]), in1=cs(j), op=ALU.mult), reads=[PID[4], tid], writes=['e0'])
                OP('dve', lambda e: e.tensor_tensor(out=vw(E[1]), in0=vw(B_[:, 0:n]), in1=sn(j), op=ALU.mult), reads=[PID[5], tid], writes=['e1'])
                OP('dve', lambda e: e.tensor_tensor(out=vw(E[2]), in0=vw(B_[:, 0:n]), in1=cs(j), op=ALU.mult), reads=[PID[5], tid], writes=['e2'])
                OP('dve', lambda e: e.tensor_tensor(out=vw(E[3]), in0=vw(A_[:, 0:n]), in1=sn(j), op=ALU.mult), reads=[PID[4], tid], writes=['e3'])
                OP('pool', lambda e: e.tensor_add(out=E[0], in0=E[0], in1=E[1]), reads=['e0', 'e1'], writes=['e0'])
                OP('pool', lambda e: e.tensor_sub(out=E[2], in0=E[2], in1=E[3]), reads=['e2', 'e3'], writes=['e2'])
                OP('pool', lambda e: e.tensor_add(out=sv(E[0])[:, :, 0], in0=sv(E[0])[:, :, 0], in1=hc[:, j, 0, 0:nseq]), reads=['e0', 'hc'], writes=['e0'])
                OP('pool', lambda e: e.tensor_add(out=sv(E[2])[:, :, 0], in0=sv(E[2])[:, :, 0], in1=hc[:, j, 1, 0:nseq]), reads=['e2', 'hc'], writes=['e2'])
                rj = rT[:, j, :, :].rearrange("p s t -> p (s t)")
                OP('dve', lambda e: e.tensor_tensor_scan(out=E[4], data0=rj, data1=E[0], initial=0.0, op0=ALU.mult, op1=ALU.add),
                   reads=['e0', tid], writes=['e4'])
                OP('dve', lambda e: e.tensor_tensor_scan(out=E[5], data0=rj, data1=E[2], initial=0.0, op0=ALU.mult, op1=ALU.add),
                   reads=['e2', tid], writes=['e5'])
                OP('pool', lambda e: e.tensor_tensor(out=vw(E[0]), in0=vw(E[4]), in1=cs(j), op=ALU.mult), reads=['e4', tid], writes=['e0'])
                OP('pool', lambda e: e.tensor_tensor(out=vw(E[1]), in0=vw(E[5]), in1=sn(j), op=ALU.mult), reads=['e5', tid], writes=['e1'])
                OP('dve', lambda e: e.tensor_tensor(out=vw(E[2]), in0=vw(E[4]), in1=sn(j), op=ALU.mult), reads=['e4', tid], writes=['e2'])
                OP('dve', lambda e: e.tensor_tensor(out=vw(E[3]), in0=vw(E[5]), in1=cs(j), op=ALU.mult), reads=['e5', tid], writes=['e3'])
                OP('pool', lambda e: e.tensor_sub(out=E[6], in0=E[0], in1=E[1]), reads=['e0', 'e1'], writes=['e6'])
                OP('pool', lambda e: e.tensor_add(out=E[7], in0=E[2], in1=E[3]), reads=['e2', 'e3'], writes=['e7'])
                OP('act', lambda e: e.copy(out=hbf[:, 0, 0:n], in_=E[6]), reads=['e6'], writes=['hbf'])
                OP('act', lambda e: e.copy(out=hbf[:, 1, 0:n], in_=E[7]), reads=['e7'], writes=['hbf'])
                OP('pool', lambda e: e.tensor_copy(out=hst[:, j, 0, :], in_=sv(E[6])[:, :, T_ - 1]), reads=['e6', hid], writes=[hid])
                OP('pool', lambda e: e.tensor_copy(out=hst[:, j, 1, :], in_=sv(E[7])[:, :, T_ - 1]), reads=['e7', hid], writes=[hid])
                Y = psum[6 + kt]
                OP('pe', lambda e: e.matmul(Y[:, 0:n], lhsT=Cmat[:, j, 0, :], rhs=hbf[:, 0, 0:n], start=(j % 4 == 0), stop=False),
                   reads=['Cmat', 'hbf'], writes=[PID[6 + kt]])
                OP('pe', lambda e: e.matmul(Y[:, 0:n], lhsT=Cmat[:, j, 1, :], rhs=hbf[:, 1, 0:n], start=False, stop=(j % 4 == 3)),
                   reads=['Cmat', 'hbf'], writes=[PID[6 + kt]])
            for kt in range(2):
                OP('dve', lambda e: e.scalar_tensor_tensor(out=ysT[:, kt, col0:col0 + n], in0=uT_f[:, kt, col0:col0 + n], scalar=sd[:, kt:kt + 1],
                                                           in1=psum[6 + kt][:, 0:n], op0=ALU.mult, op1=ALU.add), reads=['uT_f', 'sd', PID[6 + kt]], writes=['ysT'])

        def ssm_branch(c, l):
            n = c.ntok
            if c.per_seq:
                ssm_chunk(c, 0, NS, TS, cosS, sinS, rTS, 'tabS', hstS, 'hstS')
            else:
                for ch in range(NT // TC):
                    ssm_chunk(c, ch * TC, 1, TC, cosP, sinP, rTP, 'tabP', hstP, 'hstP')
            OP('act', lambda e: e.activation(out=ysT[:, :, 0:n], in_=ysT[:, :, 0:n], func=AF.Gelu), reads=['ysT'], writes=['ysT'])
            OP('act', lambda e: e.copy(out=ysb[:, :, 0:n], in_=ysT[:, :, 0:n]), reads=['ysT'], writes=['ysb'])
            for m in range(2):
                ps = psum[4 + m]
                for kt in range(2):
                    OP('pe', lambda e: e.matmul(ps[:, 0:n], lhsT=wglu_sb[:, kt, m * 128:(m + 1) * 128], rhs=ysb[:, kt, 0:n], start=(kt == 0), stop=(kt == 1)),
                       reads=['wglu_sb', 'ysb'], writes=[PID[4 + m]])
                OP('act', lambda e: e.activation(out=gsig[:, 0:n], in_=ps[:, 0:n], func=AF.Sigmoid), reads=[PID[4 + m]], writes=['gsig'])
                OP('dve', lambda e: e.tensor_tensor(out=yaT[:, m, 0:n], in0=ysT[:, m, 0:n], in1=gsig[:, 0:n], op=ALU.mult), reads=['ysT', 'gsig'], writes=['yaT'])

        CSP = 64
        dcw_sb = sb("dcw_sb", [128, 6, 4]); dpar_sb = sb("dpar_sb", [128, 72]); dnw_sb = sb("dnw_sb", [128, 1])
        nexpA = sb("nexpA", [128, 4])
        wba = sb("wba", [128, 8, 8], BF16)
        ones_bd = sb("ones_bd", [128, 128]); triI = sb("triI", [64, 64]); mStrict = sb("mStrict", [64, 64]); mInclT = sb("mInclT", [64, 64])
        OP('pool', lambda e: e.memset(ones_bd[:], 0.0), writes=['ones_bd'])
        OP('pool', lambda e: e.memset(ones_bd[0:64, 0:64], 1.0), reads=['ones_bd'], writes=['ones_bd'])
        OP('pool', lambda e: e.memset(ones_bd[64:128, 64:128], 1.0), reads=['ones_bd'], writes=['ones_bd'])
        OP('pool', lambda e: e.affine_select(out=triI[:], in_=onesf[0:64, 0:64], pattern=[[1, 64]], compare_op=ALU.is_ge, fill=0.0,
                                             base=0, channel_multiplier=-1), reads=['onesf'], writes=['triI'])
        OP('pool', lambda e: e.affine_select(out=mStrict[:], in_=onesf[0:64, 0:64], pattern=[[-1, 64]], compare_op=ALU.is_gt, fill=0.0,
                                             base=0, channel_multiplier=1), reads=['onesf'], writes=['mStrict'])
        OP('pool', lambda e: e.affine_select(out=mInclT[:], in_=onesf[0:64, 0:64], pattern=[[1, 64]], compare_op=ALU.is_ge, fill=0.0,
                                             base=0, channel_multiplier=-1), reads=['onesf'], writes=['mInclT'])
        qkvc = sb("qkvc", [128, 6, NT])
        d_kv = sb("d_kv", [64, 4, 128])
        d_ba = sb("d_ba", [64, 8]); d_g = sb("d_g", [64, 4]); d_beta = sb("d_beta", [64, 4]); d_nb = sb("d_nb", [64, 4])
        d_gc = sb("d_gc", [64, 4]); d_s4 = [sb("d_s4%d" % i, [64, 4]) for i in range(4)]
        d_r2 = sb("d_r2", [64, 4, 64]); GC_R = sb("GC_R", [128, 4, 64]); EGR = sb("EGR", [128, 4, 64])
        d_D1 = sb("d_D1", [64, 4, 64]); d_As = sb("d_As", [64, 4, 64]); d_ET = sb("d_ET", [64, 4, 64])
        d_N = sb("d_N", [64, 4, 64]); d_Q = sb("d_Q", [64, 4, 64]); d_QK = sb("d_QK", [64, 4, 64])
        d_X = sb("d_X", [64, 2, 4, 64]); d_wT = sb("d_wT", [64, 4, 64]); d_qd = sb("d_qd", [64, 4, 64]); d_qk = sb("d_qk", [64, 8, 64])
        d_kd = sb("d_kd", [64, 4, 64]); d_u = sb("d_u", [64, 4, 64]); d_o = sb("d_o", [64, 4, 64]); d_o2 = sb("d_o2", [64, 4, 64])
        Sst = sb("Sst", [64, 4, 64])
        ycT = sb("ycT", [128, 2, NT], BF16)

        def delta_setup(l):
            S.dma('sp', dcw_sb[:], dcw[l], writes=['dcw_sb'])
            S.dma('sp', dpar_sb[:], dpar[l], writes=['dpar_sb'])
            S.dma('sp', dnw_sb[:], dnw[l], writes=['dnw_sb'])
            S.dma('sp', wba[:], win_bf[l][:, C_BETA:C_BETA + 8].rearrange("(kt p) n -> p kt n", p=128), reads=wids[('win_bf', l)], writes=['wba'])
            OP('act', lambda e: e.activation(out=nexpA[:], in_=dpar_sb[:, 0:4], func=AF.Exp), reads=['dpar_sb'], writes=['nexpA'])
            OP('dve', lambda e: e.tensor_scalar(out=nexpA[:], in0=nexpA[:], scalar1=-1.0, scalar2=None, op0=ALU.mult), reads=['nexpA'], writes=['nexpA'])

        def delta_pre(c):
            n = c.ntok; T = c.T
            for j in range(6):
                acc = seqview(qkvc[:, j, 0:n], c)
                eng = 'dve'
                OP(eng, lambda e: e.tensor_scalar(out=acc, in0=c.qkv[:, j, :, 0:T], scalar1=dcw_sb[:, j, 0:1], scalar2=None, op0=ALU.mult),
                   reads=[c.qkvid, 'dcw_sb'], writes=['qkvc%d' % j])
                for i in (1, 2, 3):
                    OP(eng, lambda e, i=i: e.scalar_tensor_tensor(out=acc, in0=c.qkv[:, j, :, i:i + T], scalar=dcw_sb[:, j, i:i + 1], in1=acc,
                                                                   op0=ALU.mult, op1=ALU.add), reads=[c.qkvid, 'dcw_sb', 'qkvc%d' % j], writes=['qkvc%d' % j])
                OP('act', lambda e: e.activation(out=qkvc[:, j, 0:n], in_=qkvc[:, j, 0:n], func=AF.Silu), reads=['qkvc%d' % j], writes=['qkvc%d' % j])
            for j in range(4):
                OP('dve', lambda e: e.tensor_tensor(out=gtmp[:, 0:n], in0=qkvc[:, j, 0:n], in1=qkvc[:, j, 0:n], op=ALU.mult), reads=['qkvc%d' % j], writes=['gtmp'])
                ps = psum[4 + (j % 2)]; pid = PID[4 + (j % 2)]
                OP('pe', lambda e: e.matmul(ps[:, 0:n], lhsT=ones_bd[:], rhs=gtmp[:, 0:n], start=True, stop=True), reads=['ones_bd', 'gtmp'], writes=[pid])
                OP('dve', lambda e: e.tensor_scalar(out=gsig[:, 0:n], in0=ps[:, 0:n], scalar1=1e-6, scalar2=None, op0=ALU.add), reads=[pid], writes=['gsig'])
                OP('act', lambda e: e.activation(out=gsig[:, 0:n], in_=gsig[:, 0:n], func=AF.Sqrt), reads=['gsig'], writes=['gsig'])
                OP('dve', lambda e: e.reciprocal(out=gsig[:, 0:n], in_=gsig[:, 0:n]), reads=['gsig'], writes=['gsig'])
                sc_ = 0.125 if j < 2 else 1.0
                OP('dve', lambda e: e.scalar_tensor_tensor(out=qkvc[:, j, 0:n], in0=qkvc[:, j, 0:n], scalar=sc_, in1=gsig[:, 0:n], op0=ALU.mult, op1=ALU.mult),
                   reads=['qkvc%d' % j, 'gsig'], writes=['qkvc%d' % j])
            OP('act', lambda e: e.activation(out=zT[:, :, 0:n], in_=zT[:, :, 0:n], func=AF.Silu), reads=['zT'], writes=['zT'])

        def delta_chunk(c, col0, CS):
            cols = slice(col0, col0 + CS)
            R_ = ['dtmp']; W_ = ['dtmp']
            qk_ids = ['qkvc%d' % j for j in range(6)]
            P0 = psum[4]
            for jj in range(4):
                OP('pe', lambda e: e.matmul(P0[0:CS, jj * 128:(jj + 1) * 128], lhsT=qkvc[:, 2 + jj, cols], rhs=identf[:], start=True, stop=True),
                   reads=qk_ids + ['identf'], writes=[PID[4]])
            OP('act', lambda e: e.copy(out=d_kv[0:CS, :, :].rearrange("p a b -> p (a b)"), in_=P0[0:CS, :]), reads=[PID[4]], writes=['d_kv'])
            if DSTOP <= 1:
                return
            P1 = psum[5]
            for kt in range(8):
                OP('pe', lambda e, kt=kt: e.matmul(P1[0:CS, 0:8], lhsT=hT[:, kt, cols], rhs=wba[:, kt, :], start=(kt == 0), stop=(kt == 7)),
                   reads=['hT', 'wba'], writes=[PID[5]])
            OP('dve', lambda e: e.tensor_copy(out=d_ba[0:CS, :], in_=P1[0:CS, 0:8]), reads=[PID[5]], writes=W_)
            OP('act', lambda e: e.activation(out=d_beta[0:CS, :], in_=d_ba[0:CS, 0:4], func=AF.Sigmoid), reads=R_, writes=W_)
            OP('dve', lambda e: e.tensor_scalar(out=d_nb[0:CS, :], in0=d_beta[0:CS, :], scalar1=-1.0, scalar2=None, op0=ALU.mult), reads=R_, writes=W_)
            a0, a1, a2, a3 = [x[0:CS, :] for x in d_s4]
            OP('dve', lambda e: e.tensor_tensor(out=a0, in0=d_ba[0:CS, 4:8], in1=dpar_sb[0:CS, 4:8], op=ALU.add), reads=R_ + ['dpar_sb'], writes=W_)
            OP('act', lambda e: e.activation(out=a0, in_=a0, func=AF.Exp), reads=R_, writes=W_)
            OP('act', lambda e: e.activation(out=a0, in_=a0, func=AF.Ln, bias=1.0), reads=R_, writes=W_)
            OP('dve', lambda e: e.tensor_tensor(out=d_g[0:CS, :], in0=a0, in1=nexpA[0:CS, :], op=ALU.mult), reads=R_ + ['nexpA'], writes=W_)
            if DSTOP <= 2:
                return
            OP('pe', lambda e: e.matmul(P1[0:CS, 16:20], lhsT=triI[0:CS, 0:CS], rhs=d_g[0:CS, :], start=True, stop=True), reads=R_ + ['triI'], writes=[PID[5]])
            OP('dve', lambda e: e.tensor_copy(out=d_gc[0:CS, :], in_=P1[0:CS, 16:20]), reads=[PID[5]], writes=W_)
            OP('dve', lambda e: e.tensor_tensor(out=d_r2[0:CS, :, 0:CS], in0=d_g[0:CS, :].unsqueeze(2).to_broadcast([CS, 4, CS]),
                                                in1=triI[0:CS, 0:CS].unsqueeze(1).to_broadcast([CS, 4, CS]), op=ALU.mult), reads=R_ + ['triI'], writes=W_)
            P2 = psum[6]
            for h in range(4):
                OP('pe', lambda e: e.matmul(P2[:, h * 64:h * 64 + CS], lhsT=onesf[0:CS, 0:128], rhs=d_r2[0:CS, h, 0:CS], start=True, stop=True),
                   reads=R_ + ['onesf'], writes=[PID[6]])
            P2v = P2[:, 0:256].rearrange("p (h j) -> p h j", j=64)
            OP('dve', lambda e: e.tensor_copy(out=GC_R[:, :, 0:CS], in_=P2v[:, :, 0:CS]), reads=[PID[6]], writes=W_)
            OP('act', lambda e: e.activation(out=EGR[:, :, 0:CS], in_=GC_R[:, :, 0:CS], func=AF.Exp), reads=R_, writes=W_)
            if DSTOP <= 3:
                return
            for h in range(4):
                OP('dve', lambda e: e.tensor_scalar(out=d_D1[0:CS, h, 0:CS], in0=GC_R[0:CS, h, 0:CS], scalar1=d_gc[0:CS, h:h + 1], scalar2=None,
                                                    op0=ALU.subtract), reads=R_, writes=W_)
            OP('dve', lambda e: e.tensor_scalar(out=d_As[0:CS, :, 0:CS], in0=d_D1[0:CS, :, 0:CS], scalar1=0.0, scalar2=None, op0=ALU.max), reads=R_, writes=W_)
            OP('act', lambda e: e.activation(out=d_As[0:CS, :, 0:CS], in_=d_As[0:CS, :, 0:CS], func=AF.Exp, scale=-1.0), reads=R_, writes=W_)
            OP('dve', lambda e: e.tensor_tensor(out=d_As[0:CS, :, 0:CS], in0=d_As[0:CS, :, 0:CS],
                                                in1=mStrict[0:CS, 0:CS].unsqueeze(1).to_broadcast([CS, 4, CS]), op=ALU.mult), reads=R_ + ['mStrict'], writes=W_)
            OP('dve', lambda e: e.tensor_scalar(out=d_ET[0:CS, :, 0:CS], in0=d_D1[0:CS, :, 0:CS], scalar1=0.0, scalar2=None, op0=ALU.min), reads=R_, writes=W_)
            OP('act', lambda e: e.activation(out=d_ET[0:CS, :, 0:CS], in_=d_ET[0:CS, :, 0:CS], func=AF.Exp), reads=R_, writes=W_)
            OP('dve', lambda e: e.tensor_tensor(out=d_ET[0:CS, :, 0:CS], in0=d_ET[0:CS, :, 0:CS],
                                                in1=mInclT[0:CS, 0:CS].unsqueeze(1).to_broadcast([CS, 4, CS]), op=ALU.mult), reads=R_ + ['mInclT'], writes=W_)
            if DSTOP <= 4:
                return
            PL = psum[7]
            PLv = PL[:, 0:256].rearrange("p (a j) -> p a j", j=64)
            for a_ in range(4):
                OP('pe', lambda e: e.matmul(PLv[0:64, a_, 0:CS], lhsT=identf[:, 64:128], rhs=qkvc[:, a_, cols], start=True, stop=True),
                   reads=qk_ids + ['identf'], writes=[PID[7]])
            d_qk4 = d_qk[:, :, :].rearrange("p (a b) j -> p a b j", b=2)
            OP('dve', lambda e: e.tensor_copy(out=d_qk4[:, :, 1, 0:CS], in_=PLv[0:64, :, 0:CS]), reads=[PID[7]], writes=['d_qk'])
            OP('pool', lambda e: e.tensor_copy(out=d_qk4[:, :, 0, 0:CS], in_=qkvc[0:64, 0:4, cols]), reads=qk_ids, writes=['d_qk'])
            OP('dve', lambda e: e.tensor_copy(out=d_qk4[:, :, 0, 0:CS], in_=d_qk4[:, :, 0, 0:CS]), reads=['d_qk'], writes=['d_qk'])
            P3 = psum[7]
            for h in range(4):
                kTh = d_qk[:, 4 + h, 0:CS]; qTh = d_qk[:, h, 0:CS]
                OP('pe', lambda e: e.matmul(P3[0:CS, h * 64:h * 64 + CS], lhsT=kTh, rhs=kTh, start=True, stop=True), reads=['d_qk'], writes=[PID[7]])
                OP('pe', lambda e: e.matmul(P3[0:CS, 256 + h * 64:256 + h * 64 + CS], lhsT=kTh, rhs=qTh, start=True, stop=True), reads=['d_qk'], writes=[PID[7]])
            P3k = P3[:, 0:256].rearrange("p (h j) -> p h j", j=64); P3q = P3[:, 256:512].rearrange("p (h j) -> p h j", j=64)
            OP('dve', lambda e: e.tensor_tensor(out=d_N[0:CS, :, 0:CS], in0=P3k[0:CS, :, 0:CS], in1=d_As[0:CS, :, 0:CS], op=ALU.mult), reads=[PID[7]] + R_, writes=W_)
            OP('dve', lambda e: e.tensor_tensor(out=d_N[0:CS, :, 0:CS], in0=d_N[0:CS, :, 0:CS],
                                                in1=d_nb[0:CS, :].unsqueeze(2).to_broadcast([CS, 4, CS]), op=ALU.mult), reads=R_, writes=W_)
            OP('dve', lambda e: e.tensor_tensor(out=d_QK[0:CS, :, 0:CS], in0=P3q[0:CS, :, 0:CS], in1=d_ET[0:CS, :, 0:CS], op=ALU.mult), reads=[PID[7]] + R_, writes=W_)
            if DSTOP <= 5:
                return
            OP('act', lambda e: e.activation(out=a1, in_=d_gc[0:CS, :], func=AF.Exp), reads=R_, writes=W_)
            OP('dve', lambda e: e.tensor_tensor(out=a1, in0=a1, in1=d_beta[0:CS, :], op=ALU.mult), reads=R_, writes=W_)
            kv4 = d_kv[0:CS, :, :].rearrange("p a (b d) -> p a b d", d=64)
            k_tm = d_kv[0:CS, 0:2, :].rearrange("p a (b d) -> p (a b) d", d=64)
            v_tm = d_kv[0:CS, 2:4, :].rearrange("p a (b d) -> p (a b) d", d=64)
            OP('dve', lambda e: e.tensor_tensor(out=d_X[0:CS, 0, :, :], in0=v_tm, in1=d_beta[0:CS, :].unsqueeze(2).to_broadcast([CS, 4, 64]), op=ALU.mult),
               reads=R_ + ['d_kv'], writes=W_)
            OP('dve', lambda e: e.tensor_tensor(out=d_X[0:CS, 1, :, :], in0=k_tm, in1=a1.unsqueeze(2).to_broadcast([CS, 4, 64]), op=ALU.mult),
               reads=R_ + ['d_kv'], writes=W_)
            if DSTOP <= 6:
                return
            PA = psum[4]; PB = psum[5]
            PAv = PA[:, 0:256].rearrange("p (h j) -> p h j", j=64); PBv = PB[:, 0:256].rearrange("p (h j) -> p h j", j=64)
            for h in range(4):
                OP('pe', lambda e: e.matmul(PAv[0:CS, h, 0:CS], lhsT=d_N[0:CS, h, 0:CS], rhs=identf[0:CS, 0:CS], start=True, stop=True),
                   reads=R_ + ['identf'], writes=[PID[4]])
            OP('act', lambda e: e.copy(out=d_Q[0:CS, :, 0:CS], in_=PAv[0:CS, :, 0:CS]), reads=[PID[4]], writes=W_)
            nlev = {64: 6, 4: 2}[CS]
            PX = psum[6]
            PXv = PX[:, 0:512].rearrange("p (a h d) -> p a h d", a=2, h=4)
            for lev in range(nlev):
                for h in range(4):
                    for a in range(2):
                        OP('pe', lambda e: e.matmul(PXv[0:CS, a, h, :], lhsT=d_Q[0:CS, h, 0:CS], rhs=d_X[0:CS, a, h, :], start=True, stop=True),
                           reads=R_, writes=[PID[6]])
                OP('dve', lambda e: e.tensor_tensor(out=d_X[0:CS, :, :, :], in0=d_X[0:CS, :, :, :], in1=PXv[0:CS, :, :, :], op=ALU.add), reads=R_ + [PID[6]], writes=W_)
                if lev < nlev - 1:
                    for h in range(4):
                        OP('pe', lambda e: e.matmul(PAv[0:CS, h, 0:CS], lhsT=d_Q[0:CS, h, 0:CS], rhs=d_N[0:CS, h, 0:CS], start=True, stop=True),
                           reads=R_, writes=[PID[4]])
                        OP('pe', lambda e: e.matmul(PBv[0:CS, h, 0:CS], lhsT=d_N[0:CS, h, 0:CS], rhs=d_Q[0:CS, h, 0:CS], start=True, stop=True),
                           reads=R_, writes=[PID[5]])
                    OP('act', lambda e: e.copy(out=d_N[0:CS, :, 0:CS], in_=PAv[0:CS, :, 0:CS]), reads=[PID[4]] + R_, writes=W_)
                    OP('dve', lambda e: e.tensor_copy(out=d_Q[0:CS, :, 0:CS], in_=PBv[0:CS, :, 0:CS]), reads=[PID[5]] + R_, writes=W_)
            if DSTOP <= 7:
                return
            PW = psum[7]
            PWv = PW[:, 0:256].rearrange("p (h j) -> p h j", j=64)
            for h in range(4):
                OP('pe', lambda e: e.matmul(PWv[0:64, h, 0:CS], lhsT=d_X[0:CS, 1, h, :], rhs=identf[0:CS, 0:CS], start=True, stop=True),
                   reads=R_ + ['identf'], writes=[PID[7]])
            OP('act', lambda e: e.copy(out=d_wT[:, :, 0:CS], in_=PWv[0:64, :, 0:CS]), reads=[PID[7]], writes=W_)
            if DSTOP <= 8:
                return
            OP('pool', lambda e: e.tensor_tensor(out=d_qd[:, :, 0:CS], in0=d_qk[:, 0:4, 0:CS], in1=EGR[0:64, :, 0:CS], op=ALU.mult),
               reads=R_ + ['d_qk'], writes=['d_qd'])
            OP('dve', lambda e: e.tensor_tensor(out=a2, in0=GC_R[0:CS, :, CS - 1], in1=d_gc[0:CS, :], op=ALU.subtract), reads=R_, writes=W_)
            OP('act', lambda e: e.activation(out=a2, in_=a2, func=AF.Exp), reads=R_, writes=W_)
            OP('dve', lambda e: e.tensor_tensor(out=d_kd[0:CS, :, :], in0=k_tm, in1=a2.unsqueeze(2).to_broadcast([CS, 4, 64]), op=ALU.mult), reads=R_ + ['d_kv'], writes=W_)
            if DSTOP <= 9:
                return
            PU = psum[4]
            PUv = PU[:, 0:256].rearrange("p (h d) -> p h d", d=64)
            for h in range(4):
                OP('pe', lambda e: e.matmul(PUv[0:CS, h, :], lhsT=d_wT[:, h, 0:CS], rhs=Sst[:, h, :], start=True, stop=True),
                   reads=R_ + ['Sst'], writes=[PID[4]])
            OP('dve', lambda e: e.tensor_tensor(out=d_u[0:CS, :, :], in0=d_X[0:CS, 0, :, :], in1=PUv[0:CS, :, :], op=ALU.subtract), reads=R_ + [PID[4]], writes=W_)
            if DSTOP <= 10:
                return
            PO = psum[5]
            POv = PO[:, 0:256].rearrange("p (h d) -> p h d", d=64)
            for h in range(4):
                OP('pe', lambda e: e.matmul(POv[0:CS, h, :], lhsT=d_qd[:, h, 0:CS], rhs=Sst[:, h, :], start=True, stop=False),
                   reads=['d_qd', 'Sst'], writes=[PID[5]])
                OP('pe', lambda e: e.matmul(POv[0:CS, h, :], lhsT=d_QK[0:CS, h, 0:CS], rhs=d_u[0:CS, h, :], start=False, stop=True),
                   reads=R_, writes=[PID[5]])
            OP('act', lambda e: e.copy(out=d_o[0:CS, :, :], in_=POv[0:CS, :, :]), reads=[PID[5]], writes=['d_o'])
            if DSTOP <= 11:
                return
            PS_ = psum[6]
            PSv = PS_[:, 0:256].rearrange("p (h d) -> p h d", d=64)
            for h in range(4):
                OP('pe', lambda e: e.matmul(PSv[0:64, h, :], lhsT=d_kd[0:CS, h, :], rhs=d_u[0:CS, h, :], start=True, stop=True), reads=R_, writes=[PID[6]])
            OP('dve', lambda e: e.tensor_tensor(out=Sst[:, :, :], in0=Sst[:, :, :], in1=EGR[0:64, :, CS - 1:CS].to_broadcast([64, 4, 64]), op=ALU.mult),
               reads=['Sst'] + R_, writes=['Sst'])
            OP('dve', lambda e: e.tensor_tensor(out=Sst[:, :, :], in0=Sst[:, :, :], in1=PSv[0:64, :, :], op=ALU.add), reads=['Sst', PID[6]], writes=['Sst'])
            if DSTOP <= 12:
                return
            OP('pool', lambda e: e.tensor_tensor(out=d_o2[0:CS, :, :], in0=d_o[0:CS, :, :], in1=d_o[0:CS, :, :], op=ALU.mult), reads=['d_o'], writes=['d_o2'])
            OP('dve', lambda e: e.reduce_sum(out=a3, in_=d_o2[0:CS, :, :], axis=mybir.AxisListType.X), reads=['d_o2'], writes=W_)
            OP('dve', lambda e: e.tensor_scalar(out=a3, in0=a3, scalar1=1.0 / 64, scalar2=1e-6, op0=ALU.mult, op1=ALU.add), reads=R_, writes=W_)
            OP('act', lambda e: e.activation(out=a3, in_=a3, func=AF.Sqrt), reads=R_, writes=W_)
            OP('dve', lambda e: e.reciprocal(out=a3, in_=a3), reads=R_, writes=W_)
            OP('dve', lambda e: e.tensor_tensor(out=d_o2[0:CS, :, :], in0=d_o[0:CS, :, :], in1=a3.unsqueeze(2).to_broadcast([CS, 4, 64]), op=ALU.mult),
               reads=['d_o', 'd_o2'] + R_, writes=['d_o2'])
            PY = psum[7]
            for pr in range(2):
                OP('pe', lambda e: e.matmul(PY[:, 256 + pr * 64:256 + pr * 64 + CS], lhsT=d_o2[0:CS, 2 * pr:2 * pr + 2, :].rearrange("p a d -> p (a d)"),
                                            rhs=identf[0:CS, 0:CS], start=True, stop=True), reads=['d_o2', 'identf'], writes=[PID[7]])
                OP('dve', lambda e: e.scalar_tensor_tensor(out=ycT[:, pr, cols], in0=PY[:, 256 + pr * 64:256 + pr * 64 + CS], scalar=dnw_sb[:, 0:1],
                                                           in1=zT[:, pr, cols], op0=ALU.mult, op1=ALU.mult), reads=[PID[7], 'dnw_sb', 'zT'], writes=['ycT'])

        def delta_branch(c, l, first_tile, last_tile):
            delta_pre(c)
            if c.per_seq:
                for s_ in range(NS):
                    S.dma('sp', Sst[:], st_dl[l, s_].rearrange("h k v -> k h v"), writes=['Sst'])
                    delta_chunk(c, s_ * TS, TS)
                    S.dma('pool', o_dls[l, s_].rearrange("h k v -> k h v"), Sst[:], reads=['Sst'], is_output=True)
            else:
                if first_tile:
                    OP('pool', lambda e: e.memset(Sst[:], 0.0), reads=['Sst'], writes=['Sst'])
                for ch in range(NT // CSP):
                    delta_chunk(c, ch * CSP, CSP)
                if last_tile:
                    S.dma('pool', o_dlp[l, 0].rearrange("h k v -> k h v"), Sst[:], reads=['Sst'], is_output=True)

        NPG = 16
        actF = actT[:].rearrange("p a b -> p (a b)").bitcast(F32)
        kpg = [actF[:, 0:256], actF[:, 256:512]]
        vpg = [actF[:, 512:768], actF[:, 768:1024]]
        KTs = actF[0:64, 1024:1536].rearrange("p (h k) -> p h k", k=128)
        idx_i = actF[:, 1536:1792].bitcast(I32)
        idx_f = actF[:, 1792:2048]
        q4 = actF[0:64, 2048:2304].rearrange("p (h t) -> p h t", t=64)
        k4 = actF[0:64, 2304:2560].rearrange("p (h t) -> p h t", t=64)
        v4 = actF[0:4, 2560:2816]
        piota = sb("piota", [128, 1])
        OP('pool', lambda e: e.iota(piota[:], pattern=[[0, 1]], base=0, channel_multiplier=1, allow_small_or_imprecise_dtypes=True), writes=['piota'])
        sa_ids = ['kpg0', 'kpg1', 'vpg0', 'vpg1', 'KTs', 'idx', 'q4k4', 'v4']
        sZ = aE; sL = aL; sX = aX
        sCR = carry[:, 0, :]
        sN = carry[:, 1, :]
        trisF = ysT[:, 0, 0:128]

        def sattn_setup(l):
            OP('dve', lambda e: e.memset(idx_f[:, 0:1], 0.0), reads=['actT'], writes=sa_ids)
            S.dma('sp', idx_i, ptab, reads=['idx'], writes=['idx'])
            OP('dve', lambda e: e.tensor_copy(out=idx_f, in_=idx_i), reads=['idx'], writes=['idx'])
            OP('dve', lambda e: e.tensor_scalar(out=idx_f, in0=idx_f, scalar1=128.0, scalar2=piota[:, 0:1], op0=ALU.mult, op1=ALU.add),
               reads=['idx', 'piota'], writes=['idx'])
            OP('dve', lambda e: e.tensor_scalar(out=idx_f, in0=idx_f, scalar1=float(l * n_pool * 128), scalar2=None, op0=ALU.add), reads=['idx'], writes=['idx'])
            OP('dve', lambda e: e.tensor_copy(out=idx_i, in_=idx_f), reads=['idx'], writes=['idx'])

        def sattn(c, l):
            n = 64
            for (dst, c0) in ((q4, C_Q), (k4, C_K)):
                wb, wid = load_w('win_bf', l, win_bf[l], c0, 256)
                for h in range(4):
                    ps = psum[4 + (h % 2)]; pid = PID[4 + (h % 2)]
                    for kt in range(8):
                        OP('pe', lambda e, kt=kt: e.matmul(ps[0:64, 0:n], lhsT=wb[:, kt, h * 64:(h + 1) * 64], rhs=hT[:, kt, 0:n],
                                                          start=(kt == 0), stop=(kt == 7)), reads=[wid, 'hT'], writes=[pid])
                    OP('act', lambda e: e.copy(out=dst[:, h, :], in_=ps[0:64, 0:n]), reads=[pid], writes=['q4k4'])
            OP('pool', lambda e: e.affine_select(out=trisF, in_=onesf[:, 0:128], pattern=[[-1, 128]], compare_op=ALU.is_gt, fill=0.0,
                                                 base=0, channel_multiplier=1), reads=['onesf', 'ysT'], writes=['ysT'])
            di = {'i': 0}
            wv, wvid = load_w('win_bf', l, win_bf[l], C_V, 256)
            for s_ in range(NS):
                tcols = slice(s_ * TS, (s_ + 1) * TS)
                Zp = psum[6]
                Vn = psum[4]
                for kt in range(8):
                    OP('pe', lambda e, kt=kt: e.matmul(Vn[0:4, 256:512], lhsT=hT[:, kt, tcols], rhs=wv[:, kt, 0:256], start=(kt == 0), stop=(kt == 7)),
                       reads=[wvid, 'hT'], writes=[PID[4]])
                OP('act', lambda e: e.copy(out=v4, in_=Vn[0:4, 256:512]), reads=[PID[4]], writes=['v4'])
                for pg in range(NPG):
                    i = di['i']; di['i'] ^= 1
                    col = s_ * NPG + pg
                    S.dma_custom('pool', lambda e: e.indirect_dma_start(out=kpg[i], out_offset=None, in_=ck,
                                                                        in_offset=bass.IndirectOffsetOnAxis(ap=idx_i[:, col:col + 1], axis=0)),
                                 reads=['idx'], writes=['kpg%d' % i])
                    PT = psum[5]
                    for h in range(4):
                        OP('pe', lambda e: e.matmul(PT[0:64, h * 128:(h + 1) * 128], lhsT=kpg[i][:, h * 64:(h + 1) * 64], rhs=identf[:],
                                                    start=True, stop=True), reads=['kpg%d' % i, 'identf'], writes=[PID[5]])
                    OP('act', lambda e: e.copy(out=KTs, in_=PT[0:64, :].rearrange("p (h k) -> p h k", k=128)), reads=[PID[5]], writes=['KTs'])
                    for h in range(4):
                        OP('pe', lambda e: e.matmul(Zp[:, pg * 16 + h * 4:pg * 16 + h * 4 + 4], lhsT=KTs[:, h, :], rhs=q4[:, h, tcols],
                                                    start=True, stop=True), reads=['KTs', 'q4k4'], writes=[PID[6]])
                Zn = psum[7]
                for h in range(4):
                    OP('pe', lambda e: e.matmul(Zn[0:4, 256 + h * 4:256 + h * 4 + 4], lhsT=k4[:, h, tcols], rhs=q4[:, h, tcols], start=True, stop=True),
                       reads=['q4k4'], writes=[PID[7]])
                OP('act', lambda e: e.activation(out=sZ[:], in_=Zp[:, 0:256], func=AF.Exp, scale=-0.125), reads=[PID[6]], writes=['aE'])
                OP('act', lambda e: e.activation(out=sL[:], in_=sZ[:], func=AF.Ln, bias=1.0), reads=['aE'], writes=['aL'])
                OP('dve', lambda e: e.scalar_tensor_tensor(out=sN, in0=Zp[:, 0:256], scalar=0.125, in1=sL[:], op0=ALU.mult, op1=ALU.add),
                   reads=[PID[6], 'aL'], writes=['carry'])
                En = sX[0:4, 0:16]; Lnn = sX[0:4, 16:32]; Nn = sX[0:4, 32:48]
                OP('act', lambda e: e.activation(out=En, in_=Zn[0:4, 256:272], func=AF.Exp, scale=-0.125), reads=[PID[7]], writes=['aX'])
                OP('act', lambda e: e.activation(out=Lnn, in_=En, func=AF.Ln, bias=1.0), reads=['aX'], writes=['aX'])
                OP('dve', lambda e: e.scalar_tensor_tensor(out=Nn, in0=Zn[0:4, 256:272], scalar=0.125, in1=Lnn, op0=ALU.mult, op1=ALU.add),
                   reads=[PID[7], 'aX'], writes=['aX'])
                OP('dve', lambda e: e.tensor_tensor(out=sX[0:4, 48:52], in0=mInclT[0:4, 0:4], in1=identf[0:4, 0:4], op=ALU.subtract),
                   reads=['mInclT', 'identf', 'aX'], writes=['aX'])
                cm = sX[0:4, 48:52].unsqueeze(1).to_broadcast([4, 4, 4])
                Nn3 = Nn.rearrange("p (h t) -> p h t", t=4)
                OP('dve', lambda e: e.tensor_tensor(out=Nn3, in0=Nn3, in1=cm, op=ALU.mult), reads=['aX'], writes=['aX'])
                Tt = psum[7]
                OP('pe', lambda e: e.matmul(Tt[:, 0:256], lhsT=onesf[:, 0:128], rhs=sN, start=True, stop=True), reads=['onesf', 'carry'], writes=[PID[7]])
                OP('pe', lambda e: e.matmul(Tt[:, 272:288], lhsT=onesf[0:4, 0:128], rhs=Nn, start=True, stop=True), reads=['onesf', 'aX'], writes=[PID[7]])
                OP('pe', lambda e: e.matmul(Tt[0:4, 288:304], lhsT=trisF[0:4, 0:4], rhs=Nn, start=True, stop=True), reads=['ysT', 'aX'], writes=[PID[7]])
                Sp = psum[5]
                OP('pe', lambda e: e.matmul(Sp[:, 0:256], lhsT=trisF, rhs=sN, start=True, stop=True), reads=['ysT', 'carry'], writes=[PID[5]])
                CR3 = sCR.rearrange("p (g c) -> p g c", c=16)
                Tt3 = Tt[:, 0:256].rearrange("p (g c) -> p g c", c=16)
                OP('dve', lambda e: e.tensor_copy(out=CR3[:, NPG - 1, :], in_=Tt[:, 272:288]), reads=[PID[7]], writes=['carry'])
                for pg in range(NPG - 2, -1, -1):
                    OP('dve', lambda e, pg=pg: e.tensor_tensor(out=CR3[:, pg, :], in0=CR3[:, pg + 1, :], in1=Tt3[:, pg + 1, :], op=ALU.add),
                       reads=[PID[7], 'carry'], writes=['carry'])
                OP('dve', lambda e: e.tensor_tensor(out=sZ[:], in0=Sp[:, 0:256], in1=sCR, op=ALU.add), reads=[PID[5], 'carry'], writes=['aE'])
                OP('pool', lambda e: e.tensor_tensor(out=sZ[:], in0=sZ[:], in1=sL[:], op=ALU.add), reads=['aE', 'aL'], writes=['aE'])
                OP('act', lambda e: e.activation(out=sZ[:], in_=sZ[:], func=AF.Exp, scale=-1.0), reads=['aE'], writes=['aE'])
                Wn = sX[0:4, 64:80]
                OP('dve', lambda e: e.tensor_tensor(out=Wn, in0=Tt[0:4, 288:304], in1=Lnn, op=ALU.add), reads=[PID[7], 'aX'], writes=['aX'])
                OP('act', lambda e: e.activation(out=Wn, in_=Wn, func=AF.Exp, scale=-1.0), reads=['aX'], writes=['aX'])
                Wn3 = Wn.rearrange("p (h t) -> p h t", t=4)
                OP('dve', lambda e: e.tensor_tensor(out=Wn3, in0=Wn3, in1=cm, op=ALU.mult), reads=['aX'], writes=['aX'])
                Yp = psum[4]
                for pg in range(NPG):
                    i = di['i']; di['i'] ^= 1
                    col = s_ * NPG + pg
                    S.dma_custom('pool', lambda e: e.indirect_dma_start(out=vpg[i], out_offset=None, in_=cv,
                                                                        in_offset=bass.IndirectOffsetOnAxis(ap=idx_i[:, col:col + 1], axis=0)),
                                 reads=['idx'], writes=['vpg%d' % i])
                    for h in range(4):
                        OP('pe', lambda e: e.matmul(Yp[0:64, h * 4:h * 4 + 4], lhsT=vpg[i][:, h * 64:(h + 1) * 64], rhs=sZ[:, pg * 16 + h * 4:pg * 16 + h * 4 + 4],
                                                    start=(pg == 0 and h == 0), stop=False), reads=['vpg%d' % i, 'aE'], writes=[PID[4]])
                for h in range(4):
                    OP('pe', lambda e: e.matmul(Yp[0:64, h * 4:h * 4 + 4], lhsT=v4[:, h * 64:(h + 1) * 64],
                                                rhs=Wn[:, h * 4:h * 4 + 4], start=False, stop=(h == 3)), reads=['v4', 'aX'], writes=[PID[4]])
                OP('act', lambda e: e.copy(out=ydT[:, :, tcols], in_=Yp[0:64, 0:16].rearrange("p (h t) -> p h t", t=4)), reads=[PID[4]], writes=['ydT'])
            OP('dve', lambda e: e.memset(idx_f[:, 0:1], 0.0), reads=sa_ids, writes=['actT'])

        class Cfg:
            pass
        P_ = Cfg(); P_.nsub = 2; P_.npart = 128; P_.ntok = NT; P_.nseq = 1; P_.T = NT; P_.per_seq = False
        P_.modT = modT_p; P_.modid = 'modT_p'; P_.G = Gp; P_.Gid = 'G'; P_.sT = sTp; P_.sTid = 'sTp'; P_.qkv = qkvp; P_.qkvid = 'qkvp'
        S_ = Cfg(); S_.nsub = 1; S_.npart = 64; S_.ntok = 64; S_.nseq = NS; S_.T = TS; S_.per_seq = True
        S_.modT = modT_s; S_.modid = 'modT_s'; S_.G = Gs; S_.Gid = 'G'; S_.sT = sTs; S_.sTid = 'sTs'; S_.qkv = qkvs; S_.qkvid = 'qkvs'

        cur = {'l': 0}

        def seqview(ap2d, c):
            return ap2d.rearrange("p (s t) -> p s t", t=c.T)

        def ln_stats_norm(c, s, src, dst, affine=None):
            npart = c.npart
            OP('dve', lambda e: e.bn_stats(out=stats[:npart, 0, :], in_=src[:, 0:512]), reads=['xt', 'tt'], writes=['stats'])
            OP('dve', lambda e: e.bn_stats(out=stats[:npart, 1, :], in_=src[:, 512:1024]), reads=['xt', 'tt'], writes=['stats'])
            OP('dve', lambda e: e.bn_aggr(out=mv[:npart, :], in_=stats[:npart, :, :]), reads=['stats'], writes=['mv'])
            OP('dve', lambda e: e.tensor_scalar(out=rstd[:npart, :], in0=mv[:npart, 1:2], scalar1=1e-5, scalar2=None, op0=ALU.add),
               reads=['mv'], writes=['rstd'])
            OP('act', lambda e: e.activation(out=rstd[:npart, :], in_=rstd[:npart, :], func=AF.Sqrt), reads=['rstd'], writes=['rstd'])
            OP('dve', lambda e: e.reciprocal(out=rstd[:npart, :], in_=rstd[:npart, :]), reads=['rstd'], writes=['rstd'])
            OP('dve', lambda e: e.tensor_scalar(out=dst, in0=src, scalar1=mv[:npart, 0:1], scalar2=rstd[:npart, 0:1],
                                                op0=ALU.subtract, op1=ALU.mult), reads=['xt', 'tt', 'mv', 'rstd'], writes=['xt', 'xn', 'tt'])
            if affine is not None:
                gi, bi = affine
                if s == 0:
                    S.dma('sp', LNP[:], lnp[cur['l'], :, gi:gi + 2, :], writes=['LNP'])
                gi, bi = 0, 1
                OP('dve', lambda e: e.tensor_tensor(out=dst, in0=dst, in1=LNP[:npart, gi, :], op=ALU.mult), reads=['xt', 'LNP'], writes=['xt'])
                OP('dve', lambda e: e.tensor_tensor(out=dst, in0=dst, in1=LNP[:npart, bi, :], op=ALU.add), reads=['xt', 'LNP'], writes=['xt'])

        def ln_to_hT(c, off_scale, off_shift):
            npart = c.npart
            for s in range(c.nsub):
                ln_stats_norm(c, s, xt[:npart, s, :], xn[:npart, s, :])
                pb = psum_bf[2 + (s % 2)]; pid = PID[2 + (s % 2)]
                for kt in range(8):
                    OP('pe', lambda e, kt=kt: e.transpose(out=pb[:, kt * 128:kt * 128 + npart], in_=xn[:npart, s, kt * 128:(kt + 1) * 128],
                                                         identity=ident[:npart, :npart]), reads=['xn', 'ident'], writes=[pid])
                for kt in range(8):
                    if not c.per_seq:
                        OP('act', lambda e, kt=kt: e.activation(out=hT[:, kt, s * npart:(s + 1) * npart], in_=pb[:, kt * 128:kt * 128 + npart],
                                                               func=AF.Identity, scale=c.modT[:, off_scale + kt, 0:1],
                                                               bias=c.modT[:, off_shift + kt, 0:1]), reads=[pid, c.modid], writes=['hT'])
                    else:
                        src = seqview(pb[:, kt * 128:kt * 128 + npart], c)
                        dst = seqview(hT[:, kt, 0:npart], c)
                        OP('dve', lambda e, kt=kt: e.tensor_tensor(
                            out=dst, in0=src, in1=c.modT[:, off_scale + kt, :].unsqueeze(2).to_broadcast([128, NS, TS]), op=ALU.mult),
                            reads=[pid, c.modid], writes=['hT'])
                        OP('dve', lambda e, kt=kt: e.tensor_tensor(
                            out=dst, in0=dst, in1=c.modT[:, off_shift + kt, :].unsqueeze(2).to_broadcast([128, NS, TS]), op=ALU.add),
                            reads=['hT', c.modid], writes=['hT'])

        evac_rr = {'i': 0}

        def evac(out, in_, reads, writes):
            evac_rr['i'] ^= 1
            if evac_rr['i']:
                OP('act', lambda e: e.copy(out=out, in_=in_), reads=reads, writes=writes)
            else:
                OP('dve', lambda e: e.tensor_copy(out=out, in_=in_), reads=reads, writes=writes)

        def inproj(c, l):
            n = c.ntok
            pi = {'i': 0}

            def fm_tile(wb, wid, cl, dst_fn):
                b = 4 + (pi['i'] % 4); pi['i'] += 1
                ps = psum[b]; pid = PID[b]
                for kt in range(8):
                    OP('pe', lambda e, kt=kt: e.matmul(ps[:, 0:n], lhsT=wb[:, kt, cl:cl + 128], rhs=hT[:, kt, 0:n],
                                                      start=(kt == 0), stop=(kt == 7)), reads=[wid, 'hT'], writes=[pid])
                dst_fn(ps[:, 0:n], pid)
            for g in range(int(os.environ.get('INPG', '4'))):
                wb, wid = load_w('win_bf', l, win_bf[l], g * 512, 512)
                for m in range(4):
                    mt = g * 4 + m
                    if mt < 2:
                        def dst_fn(p, pid, mt=mt):
                            OP('dve', lambda e: e.tensor_copy(out=uT_f[:, mt, 0:n], in_=p), reads=[pid], writes=['uT_f'])
                            OP('act', lambda e: e.copy(out=uT_bf[:, mt, 0:n], in_=uT_f[:, mt, 0:n]), reads=['uT_f'], writes=['uT_bf'])
                    elif mt < 4:
                        def dst_fn(p, pid, mt=mt):
                            evac(bT[:, mt - 2, 0:n], p, [pid], ['bT'])
                    elif mt < 8:
                        def dst_fn(p, pid, mt=mt):
                            evac(cxT[:, mt - 4, 0:n], p, [pid], ['cxT'])
                    elif mt < 14:
                        def dst_fn(p, pid, mt=mt):
                            evac(c.qkv[:, mt - 8, :, 3:3 + c.T], seqview(p, c), [pid], [c.qkvid])
                    else:
                        def dst_fn(p, pid, mt=mt):
                            evac(zT[:, mt - 14, 0:n], p, [pid], ['zT'])
                    fm_tile(wb, wid, m * 128, dst_fn)
            wb, wid = load_w('win_bf', l, win_bf[l], C_Q, 512)
            for m in range(int(os.environ.get('INPQ', '4'))):
                def dst_fn(p, pid, m=m):
                    if m < 2:
                        evac(qT[:, m, 0:n], p, [pid], ['qT'])
                    else:
                        evac(kT[:, m - 2, 0:n], p, [pid], ['kT'])
                fm_tile(wb, wid, m * 128, dst_fn)

        def kv_out(c, l, o_k, o_v, tok0):
            npart = c.npart
            wb, wid = load_w('win_bf', l, win_bf[l], C_K, 512)
            for s in range(c.nsub):
                ps = psum[s % 2]; pid = PID[s % 2]
                for kt in range(8):
                    OP('pe', lambda e, kt=kt: e.matmul(ps[:npart, :], lhsT=hT[:, kt, s * npart:(s + 1) * npart], rhs=wb[:, kt, 0:512],
                                                      start=(kt == 0), stop=(kt == 7)), reads=[wid, 'hT'], writes=[pid])
                OP('act', lambda e: e.copy(out=kv_tm[:npart, s, :], in_=ps[:npart, :]), reads=[pid], writes=['kv_tm%d' % s])
                r0 = tok0 + s * npart
                S.dma('pool', o_k[l, r0:r0 + npart, :], kv_tm[:npart, s, 0:256], reads=['kv_tm%d' % s], is_output=True)
                S.dma('pool', o_v[l, r0:r0 + npart, :], kv_tm[:npart, s, 256:512], reads=['kv_tm%d' % s], is_output=True)

        def kv_to_scratch(c, l, t):
            tok0 = t * NT
            for pr in range(2):
                S.dma('pool', KT_d[pr, :, tok0:tok0 + NT], kT[:, pr, 0:NT], reads=['kT'], writes=['KTd%d_%d' % (pr, t)])
            for s_ in range(2):
                OP('act', lambda e: e.copy(out=vbf[:, s_, :], in_=kv_tm[:, s_, 256:512]), reads=['kv_tm%d' % s_], writes=['vbf'])
            S.dma('pool', V_d[tok0:tok0 + NT, :].rearrange("(b p) c -> p b c", p=128), vbf[:], reads=['vbf'], writes=['Vd%d' % t])

        def attn_prompt(c, l, t):
            ui = {'i': 0}
            for pr in range(2):
                OP('pool', lambda e: e.memset(carry[:], 0.0), writes=['carry'])
                nblk = 2 * t + 2
                done = 0
                for u in range(t, -1, -1):
                    i = ui['i']; ui['i'] ^= 1
                    S.dma('sp', kch[i][:], KT_d[pr, :, u * NT:(u + 1) * NT], reads=['KTd%d_%d' % (pr, u)], writes=['kch%d' % i])
                    S.dma('sp', vch[i][:], V_d[u * NT:(u + 1) * NT, pr * 128:(pr + 1) * 128].rearrange("(b p) c -> p b c", p=128),
                          reads=['Vd%d' % u], writes=['vch%d' % i])
                    for kbl in (1, 0):
                        kb = 2 * u + kbl
                        diag = kb - 2 * t if kb >= 2 * t else None
                        for hl in range(2):
                            b0 = 64 * hl
                            zb = 4 + ((done * 2 + hl) % 2)
                            Z = psum[zb]; zid = PID[zb]
                            OP('pe', lambda e: e.matmul(Z[:, 0:NT], lhsT=kch[i][b0:b0 + 64, kbl * 128:(kbl + 1) * 128], rhs=qT[b0:b0 + 64, pr, 0:NT],
                                                        start=True, stop=True), reads=['kch%d' % i, 'qT'], writes=[zid])
                            OP('act', lambda e: e.activation(out=aE[:], in_=Z[:, 0:NT], func=AF.Exp, scale=-0.125), reads=[zid], writes=['aE'])
                            OP('act', lambda e: e.activation(out=aL[:], in_=aE[:], func=AF.Ln, bias=1.0), reads=['aE'], writes=['aL'])
                            OP('dve', lambda e: e.scalar_tensor_tensor(out=aN[:], in0=Z[:, 0:NT], scalar=0.125, in1=aL[:], op0=ALU.mult, op1=ALU.add),
                               reads=[zid, 'aL'], writes=['aN'])
                            if diag is not None:
                                OP('dve', lambda e: e.tensor_tensor(out=aN[:], in0=aN[:], in1=amaskb[:, diag, :], op=ALU.mult),
                                   reads=['aN', 'amaskb'], writes=['aN'])
                            OP('pe', lambda e: e.matmul(psum[6][:, 0:NT], lhsT=trisb[:], rhs=aN[:], start=True, stop=True),
                               reads=['trisb', 'aN'], writes=[PID[6]])
                            OP('pe', lambda e: e.matmul(psum[7][:, 0:NT], lhsT=onesb[:], rhs=aN[:], start=True, stop=True),
                               reads=['onesb', 'aN'], writes=[PID[7]])
                            OP('dve', lambda e: e.tensor_tensor(out=aX[:], in0=psum[6][:, 0:NT], in1=carry[:, hl, :], op=ALU.add),
                               reads=[PID[6], 'carry'], writes=['aX'])
                            OP('pool', lambda e: e.tensor_tensor(out=aX[:], in0=aX[:], in1=aL[:], op=ALU.add), reads=['aX', 'aL'], writes=['aX'])
                            OP('act', lambda e: e.activation(out=aW[:], in_=aX[:], func=AF.Exp, scale=-1.0), reads=['aX'], writes=['aW'])
                            if diag is not None:
                                OP('pool', lambda e: e.tensor_tensor(out=aW[:], in0=aW[:], in1=amaskb[:, diag, :], op=ALU.mult),
                                   reads=['aW', 'amaskb'], writes=['aW'])
                            OP('dve', lambda e: e.tensor_tensor(out=carry[:, hl, :], in0=carry[:, hl, :], in1=psum[7][:, 0:NT], op=ALU.add),
                               reads=[PID[7], 'carry'], writes=['carry'])
                            OP('pe', lambda e: e.matmul(psum[hl][0:64, 0:NT], lhsT=vch[i][:, kbl, b0:b0 + 64], rhs=aW[:],
                                                        start=(done == 0), stop=(done == nblk - 1)), reads=['vch%d' % i, 'aW'], writes=[PID[hl]])
                        done += 1
                for hl in range(2):
                    OP('act', lambda e: e.copy(out=ydT[:, pr * 2 + hl, 0:NT], in_=psum[hl][0:64, 0:NT]), reads=[PID[hl]], writes=['ydT'])

        def conv_b(c, l):
            n = c.ntok; T = c.T
            for j in range(2):
                OP('dve', lambda e: e.tensor_tensor(out=c.sT[:, j, :, 2:2 + T], in0=seqview(cxT[:, j, 0:n], c), in1=seqview(cxT[:, 2 + j, 0:n], c),
                                                    op=ALU.mult), reads=['cxT'], writes=[c.sTid])
                acc = seqview(cacc[:, 0:n], c)
                OP('dve', lambda e: e.tensor_scalar(out=acc, in0=c.sT[:, j, :, 0:T], scalar1=cbw[:, j, 0:1], scalar2=None, op0=ALU.mult),
                   reads=[c.sTid, 'cbw'], writes=['cacc'])
                for i in (1, 2):
                    OP('dve', lambda e, i=i: e.scalar_tensor_tensor(out=acc, in0=c.sT[:, j, :, i:i + T], scalar=cbw[:, j, i:i + 1], in1=acc,
                                                                   op0=ALU.mult, op1=ALU.add), reads=[c.sTid, 'cbw', 'cacc'], writes=['cacc'])
                OP('dve', lambda e: e.tensor_tensor(out=ybT[:, j, 0:n], in0=cacc[:, 0:n], in1=bT[:, j, 0:n], op=ALU.mult),
                   reads=['cacc', 'bT'], writes=['ybT'])

        def roll_hist(c):
            T = c.T
            OP('pool', lambda e: e.tensor_copy(out=c.sT[:, :, :, 0:2], in_=c.sT[:, :, :, T:T + 2]), reads=[c.sTid], writes=[c.sTid])
            OP('pool', lambda e: e.tensor_copy(out=c.qkv[:, :, :, 0:3], in_=c.qkv[:, :, :, T:T + 3]), reads=[c.qkvid], writes=[c.qkvid])

        def merge_wo(c, l, branches):
            n = c.ntok; npart = c.npart
            first = [True] * 8
            for (bn, rhs_fn, nk, kp, breads) in branches:
                for g in range(2):
                    wb, wid = load_w('wgate_bf', l, wgate_bf[l], bn * 1024 + g * 512, 512)
                    if kp == 128:
                        srcw = wbr_bf[l, bn * 256:(bn + 1) * 256, g * 512:(g + 1) * 512].rearrange("(kt p) n -> p kt n", p=128)
                        S.dma('sp', wbr[:, 0:2, :], srcw, reads=wids[('wbr_bf', l)], writes=['wbr'])
                    else:
                        srcw = wbr_bf[l, bn * 256:(bn + 1) * 256, g * 512:(g + 1) * 512].rearrange("(h p) n -> p h n", p=64)
                        S.dma('sp', wbr[0:64, :, :], srcw, reads=wids[('wbr_bf', l)], writes=['wbr'])
                    for m4 in range(4):
                        m = g * 4 + m4
                        psg = psum[4 + (m % 2)]; pidg = PID[4 + (m % 2)]
                        psb = psum[6 + (m % 2)]; pidb = PID[6 + (m % 2)]
                        for kt in range(8):
                            OP('pe', lambda e, kt=kt: e.matmul(psg[:, 0:n], lhsT=wb[:, kt, m4 * 128:(m4 + 1) * 128], rhs=hT[:, kt, 0:n],
                                                              start=(kt == 0), stop=(kt == 7)), reads=[wid, 'hT'], writes=[pidg])
                        OP('act', lambda e: e.activation(out=gsig[:, 0:n], in_=psg[:, 0:n], func=AF.Sigmoid), reads=[pidg], writes=['gsig'])
                        for k in range(nk):
                            if kp == 128:
                                lw = wbr[:, k, m4 * 128:(m4 + 1) * 128]
                            else:
                                lw = wbr[0:64, k, m4 * 128:(m4 + 1) * 128]
                            OP('pe', lambda e, k=k, lw=lw: e.matmul(psb[:, 0:n], lhsT=lw, rhs=rhs_fn(k), start=(k == 0), stop=(k == nk - 1)),
                               reads=['wbr'] + breads, writes=[pidb])
                        if first[m]:
                            OP('dve', lambda e: e.tensor_tensor(out=mixT[:, m, 0:n], in0=gsig[:, 0:n], in1=psb[:, 0:n], op=ALU.mult),
                               reads=['gsig', pidb], writes=['mixT'])
                            first[m] = False
                        else:
                            OP('dve', lambda e: e.tensor_tensor(out=gtmp[:, 0:n], in0=gsig[:, 0:n], in1=psb[:, 0:n], op=ALU.mult),
                               reads=['gsig', pidb], writes=['gtmp'])
                            OP('dve', lambda e: e.tensor_tensor(out=mixT[:, m, 0:n], in0=mixT[:, m, 0:n], in1=gtmp[:, 0:n], op=ALU.add),
                               reads=['gtmp', 'mixT'], writes=['mixT'])
            OP('act', lambda e: e.copy(out=mixbf[:, :, 0:n], in_=mixT[:, :, 0:n]), reads=['mixT'], writes=['mixbf'])
            wbs = [load_w('wo_bf', l, wo_bf[l], h * 512, 512) for h in range(2)]
            for s in range(c.nsub):
                for h in range(2):
                    wb, wid = wbs[h]
                    ps = psum[h]; pid = PID[h]
                    for kt in range(8):
                        OP('pe', lambda e, kt=kt: e.matmul(ps[:npart, :], lhsT=mixbf[:, kt, s * npart:(s + 1) * npart], rhs=wb[:, kt, :],
                                                          start=(kt == 0), stop=(kt == 7)), reads=[wid, 'mixbf'], writes=[pid])
                    OP('dve', lambda e: e.tensor_tensor(out=tt[:npart, h * 512:(h + 1) * 512], in0=ps[:npart, :],
                                                        in1=c.G[:npart, 0, h * 512:(h + 1) * 512], op=ALU.mult), reads=[pid, c.Gid], writes=['tt'])
                OP('dve', lambda e: e.scalar_tensor_tensor(out=tt[:npart, :], in0=xt[:npart, s, :], scalar=ALPHA, in1=tt[:npart, :],
                                                           op0=ALU.mult, op1=ALU.add), reads=['xt', 'tt'], writes=['tt'])
                ln_stats_norm(c, s, tt[:npart, :], xt[:npart, s, :], affine=(0, 1))

        def ffn(c, l):
            n = c.ntok; npart = c.npart
            ln_to_hT(c, 32, 24)
            groups = [(g * 4, 4) for g in range(5)] + [(20, 2)]
            for (m0, nm) in groups:
                wa, wida = load_w('wup_bf', l, wup_bf[l], m0 * 128, nm * 128)
                wb_, widb = load_w('wup_bf', l, wup_bf[l], D_FF + m0 * 128, nm * 128)
                for mi in range(nm):
                    m = m0 + mi
                    psa = psum[4 + (m % 2)]; pida = PID[4 + (m % 2)]
                    psb = psum[6 + (m % 2)]; pidb = PID[6 + (m % 2)]
                    for kt in range(8):
                        OP('pe', lambda e, kt=kt: e.matmul(psa[:, 0:n], lhsT=wa[:, kt, mi * 128:(mi + 1) * 128], rhs=hT[:, kt, 0:n],
                                                          start=(kt == 0), stop=(kt == 7)), reads=[wida, 'hT'], writes=[pida])
                    for kt in range(8):
                        OP('pe', lambda e, kt=kt: e.matmul(psb[:, 0:n], lhsT=wb_[:, kt, mi * 128:(mi + 1) * 128], rhs=hT[:, kt, 0:n],
                                                          start=(kt == 0), stop=(kt == 7)), reads=[widb, 'hT'], writes=[pidb])
                    OP('act', lambda e: e.activation(out=gsig[:, 0:n], in_=psa[:, 0:n], func=AF.Silu), reads=[pida], writes=['gsig'])
                    OP('dve', lambda e: e.tensor_tensor(out=actT[:, m, 0:n], in0=gsig[:, 0:n], in1=psb[:, 0:n], op=ALU.mult),
                       reads=['gsig', pidb], writes=['actT'])
            for h in range(2):
                chunks = [load_w('wdown_bf', l, wdown_bf[l], h * 512, 512, kts=kts, k0=k0) for (k0, kts) in ((0, 8), (8, 8), (16, 6))]
                for s in range(c.nsub):
                    ps = psum[s]; pid = PID[s]
                    for kt in range(22):
                        wb, wid = chunks[kt // 8]
                        OP('pe', lambda e, kt=kt: e.matmul(ps[:npart, :], lhsT=actT[:, kt, s * npart:(s + 1) * npart], rhs=wb[:, kt % 8, :],
                                                          start=(kt == 0), stop=(kt == 21)), reads=[wid, 'actT'], writes=[pid])
                    OP('dve', lambda e: e.tensor_tensor(out=kv_tm[:npart, s, :], in0=ps[:npart, :],
                                                        in1=c.G[:npart, 1, h * 512:(h + 1) * 512], op=ALU.mult), reads=[pid, c.Gid], writes=['kv_tm%d' % s])
                    OP('dve', lambda e: e.scalar_tensor_tensor(out=xt[:npart, s, h * 512:(h + 1) * 512], in0=xt[:npart, s, h * 512:(h + 1) * 512],
                                                               scalar=ALPHA, in1=kv_tm[:npart, s, :], op0=ALU.mult, op1=ALU.add),
                       reads=['xt', 'kv_tm%d' % s], writes=['xt'])
            for s in range(c.nsub):
                ln_stats_norm(c, s, xt[:npart, s, :], xt[:npart, s, :], affine=(2, 3))

        def layer_tile(c, l, xsrc, xdst, o_k, o_v, tok0, first_tile, last_tile, o_cb, o_cd, xid):
            npart = c.npart
            S.dma('sp', xt[:npart, 0:c.nsub, :], xsrc, reads=[xid], writes=['xt'])
            ln_to_hT(c, 8, 0)
            if 'inproj' in STG:
                inproj(c, l)
            if 'kv' in STG:
                kv_out(c, l, o_k, o_v, tok0)
            if 'conv' in STG:
                conv_b(c, l)
            branches = [(1, lambda k: ybT[:, k, 0:c.ntok], 2, 128, ['ybT'])]
            if WITH_A:
                ssm_branch(c, l)
                branches.append((0, lambda k: yaT[:, k, 0:c.ntok], 2, 128, ['yaT']))
                if last_tile:
                    S.dma('pool', (o_sss if c.per_seq else o_ssp)[l], (hstS if c.per_seq else hstP)[:], reads=['hstS' if c.per_seq else 'hstP'], is_output=True)
            if WITH_C:
                delta_branch(c, l, first_tile, last_tile)
                branches.append((2, lambda k: ycT[:, k, 0:c.ntok], 2, 128, ['ycT']))
            if WITH_D and c.per_seq:
                sattn_setup(l)
                sattn(c, l)
                branches.append((3, lambda k: ydT[0:64, k, 0:c.ntok], 4, 64, ['ydT']))
            if WITH_D and not c.per_seq:
                kv_to_scratch(c, l, tok0 // NT)
                attn_prompt(c, l, tok0 // NT)
                branches.append((3, lambda k: ydT[0:64, k, 0:c.ntok], 4, 64, ['ydT']))
            if 'merge' in STG:
                merge_wo(c, l, branches)
            if 'ffn' in STG:
                ffn(c, l)
            if 'xout' in STG:
                S.dma('pool', xdst, xt[:npart, 0:c.nsub, :], reads=['xt'], writes=[xid], is_output=True)
            if last_tile and 'conv' in STG:
                T = c.T
                S.dma('pool', o_cb[l], c.sT[:, :, :, T:T + 2] if c.per_seq else c.sT[:, :, 0, T:T + 2], reads=[c.sTid], is_output=True)
                S.dma('pool', o_cd[l], c.qkv[:, :, :, T:T + 3] if c.per_seq else c.qkv[:, :, 0, T:T + 3], reads=[c.qkvid], is_output=True)
            elif not c.per_seq:
                roll_hist(c)

        xpt = xp.rearrange("(n s p) d -> n p s d", s=2, p=128)
        xmpt = xmid_p.rearrange("(n s p) d -> n p s d", s=2, p=128)
        oypt = o_yp.rearrange("(n s p) d -> n p s d", s=2, p=128)
        view_s = lambda a: a.rearrange("(s p) d -> p s d", s=1)
        for l in range(N_LAYERS):
            cur['l'] = l
            ada(l)
            if WITH_A:
                ssm_setup(l)
            if WITH_C:
                delta_setup(l)
            last = (l == DEPTH - 1)
            make_G(True)
            S.dma('sp', sTs[:, :, :, 0:2], st_cb[l], writes=['sTs'])
            S.dma('sp', qkvs[:, :, :, 0:3], st_cd[l], writes=['qkvs'])
            layer_tile(S_, l, view_s(xs if l == 0 else xmid_s), view_s(o_ys if last else xmid_s), o_ks, o_vs, 0, True, True, o_cbs, o_cds, 'xmid_s')
            make_G(False)
            OP('pool', lambda e: e.memset(sTp[:, :, :, 0:2], 0.0), writes=['sTp'])
            OP('pool', lambda e: e.memset(qkvp[:, :, :, 0:3], 0.0), writes=['qkvp'])
            ntile = NTILES
            for t in range(ntile):
                layer_tile(P_, l, (xpt if l == 0 else xmpt)[t], (oypt if last else xmpt)[t], o_kp, o_vp, t * NT, t == 0, t == ntile - 1, o_cbp, o_cdp, 'xmid_p%d' % t)
        print('SBUF bytes remaining', nc.sbuf_bytes_remaining)
        S.finish()
    return nc


def _prep_core(c, inp):
    b = c % 2
    sl = slice(NS * c, NS * (c + 1))
    A = np.ascontiguousarray
    m = {}
    m["xp"] = A(inp["x_prompt"][b])
    m["xs"] = A(inp["x_sample"][sl].reshape(NS * TS, D))
    m["cTp"] = A(inp["c_prompt"][b].reshape(8, 128).T.reshape(128, 8, 1))
    m["cTs"] = A(inp["c_sample"][sl].reshape(NS, 8, 128).transpose(2, 1, 0))
    m["w_ada"] = inp["w_ada"]
    m["b_adaT"] = A(inp["b_ada"].reshape(DEPTH, 48, 128).transpose(0, 2, 1))
    m["w_in"] = inp["w_in"]
    m["w_gate"] = inp["w_gate"]
    m["w_branch"] = A(inp["w_branch"].reshape(DEPTH, 4 * 256, D))
    m["w_o"] = inp["w_o"]
    m["w_up"] = inp["w_ffn_up"]
    m["w_down"] = inp["w_ffn_down"]
    lnp = np.stack([inp["ln1_g"], inp["ln1_b"], inp["ln2_g"], inp["ln2_b"]], axis=1)
    m["lnp"] = A(np.broadcast_to(lnp[:, None, :, :], (DEPTH, 128, 4, D)))
    m["convbw"] = A(inp["conv_b_w"].reshape(DEPTH, 3, 2, 128).transpose(0, 3, 2, 1))
    def gp(x):
        sh = x.shape
        return x.reshape(DEPTH, 8, 2, 64, *sh[3:]).transpose(0, 2, 3, 1, *range(4, 4 + len(sh) - 3)).reshape(DEPTH, 128, 8, *sh[3:])
    ldt = np.broadcast_to(inp["ssm_log_dt"][:, :, None], (DEPTH, 16, 64))
    m["ssmw"] = A(np.concatenate([gp(inp["ssm_a_re"])[..., None], gp(inp["ssm_a_im"])[..., None], gp(A(ldt))[..., None],
                                  gp(inp["ssm_b_re"]), gp(inp["ssm_b_im"]),
                                  gp(A(inp["ssm_c_re"].transpose(0, 1, 3, 2))), gp(A(inp["ssm_c_im"].transpose(0, 1, 3, 2)))], axis=-1))
    m["ssmd"] = A(inp["ssm_d"].reshape(DEPTH, 2, 128).transpose(0, 2, 1))
    st = np.stack([inp["state_ssm_re"][:, sl], inp["state_ssm_im"][:, sl]], axis=-1)
    m["st_ssm"] = A(gp(A(st.transpose(0, 2, 3, 4, 1))))
    m["w_glu"] = inp["w_glu"]
    m["dcw"] = A(inp["delta_conv_w"].reshape(DEPTH, 4, 6, 128).transpose(0, 3, 2, 1))
    dp = np.concatenate([inp["delta_a_log"], inp["delta_dt_bias"], inp["delta_norm_w"]], axis=1)
    m["dpar"] = A(np.broadcast_to(dp[:, None, :], (DEPTH, 128, 72)))
    m["dnw"] = A(np.concatenate([inp["delta_norm_w"], inp["delta_norm_w"]], axis=1).reshape(DEPTH, 128, 1))
    m["st_dl"] = A(inp["state_delta"][:, sl])
    npool = inp["cache_k"].shape[1]
    m["ck"] = inp["cache_k"].reshape(DEPTH * npool * 128, 256)
    m["cv"] = inp["cache_v"].reshape(DEPTH * npool * 128, 256)
    m["ptab"] = A(np.broadcast_to(inp["page_table"][sl].reshape(1, NS * 16), (128, NS * 16)).astype(np.int32))
    m["st_cb"] = A(inp["state_conv_b"][:, sl].reshape(DEPTH, NS, 2, 2, 128).transpose(0, 4, 3, 1, 2))
    m["st_cd"] = A(inp["state_conv_delta"][:, sl].reshape(DEPTH, NS, 3, 6, 128).transpose(0, 4, 3, 1, 2))
    return m


def kernel(**inp):
    inp = {k: np.asarray(v) for k, v in inp.items()}
    n_pool = inp["cache_k"].shape[1]
    nc = build_program(n_pool)
    in_maps = [_prep_core(c, inp) for c in range(8)]
    res = run_bass_kernel_spmd(nc, in_maps, core_ids=list(range(8)))
    return assemble(res.results)


def assemble(R):
    A_ = np.ascontiguousarray
    B = 2
    f = np.float32
    nco = len(R)
    nb = min(B, nco)
    y_p = np.stack([R[b]["o_yp"] for b in range(nb)], axis=0)
    y_s = np.concatenate([R[c]["o_ys"].reshape(NS, TS, D) for c in range(nco)], axis=0)
    k_p = np.stack([R[b]["o_kp"] for b in range(nb)], axis=1).reshape(DEPTH, nb, L, 4, 64)
    v_p = np.stack([R[b]["o_vp"] for b in range(nb)], axis=1).reshape(DEPTH, nb, L, 4, 64)
    k_s = np.concatenate([R[c]["o_ks"].reshape(DEPTH, NS, TS, 4, 64) for c in range(nco)], axis=1)
    v_s = np.concatenate([R[c]["o_vs"].reshape(DEPTH, NS, TS, 4, 64) for c in range(nco)], axis=1)
    cb_p = np.stack([R[b]["o_cbp"].transpose(0, 3, 2, 1).reshape(DEPTH, 2, 256) for b in range(nb)], axis=1)
    cb_s = np.concatenate([R[c]["o_cbs"].transpose(0, 3, 4, 2, 1).reshape(DEPTH, NS, 2, 256) for c in range(nco)], axis=1)
    cd_p = np.stack([R[b]["o_cdp"].transpose(0, 3, 2, 1).reshape(DEPTH, 3, 768) for b in range(nb)], axis=1)
    cd_s = np.concatenate([R[c]["o_cds"].transpose(0, 3, 4, 2, 1).reshape(DEPTH, NS, 3, 768) for c in range(nco)], axis=1)
    def ungp(x):
        sh = x.shape
        return x.reshape(DEPTH, 2, 64, 8, *sh[3:]).transpose(0, 3, 1, 2, *range(4, 4 + len(sh) - 3)).reshape(DEPTH, 16, 64, *sh[3:])
    ssp = np.stack([ungp(R[b]["o_ssp"])[..., 0] for b in range(nb)], axis=1)
    sss = np.concatenate([ungp(R[c]["o_sss"]).transpose(0, 4, 1, 2, 3) for c in range(nco)], axis=1)
    d_p = np.stack([R[b]["o_dlp"][:, 0].reshape(DEPTH, 4, 64, 64) for b in range(nb)], axis=1)
    d_s = np.concatenate([R[c]["o_dls"].reshape(DEPTH, NS, 4, 64, 64) for c in range(nco)], axis=1)
    z = lambda *sh: np.zeros(sh, f)
    ns = NS * nco
    return (y_p, y_s, k_p, v_p, k_s, v_s,
            A_(ssp[..., 0]), A_(ssp[..., 1]), A_(sss[..., 0]), A_(sss[..., 1]),
            cb_p, cb_s, d_p, d_s, cd_p, cd_s)
```
